# Optimizing a Trainium2 kernel written in Bass

```python
import jax, jax.numpy as jnp
from jax import lax
import numpy as np

D_MODEL = 2048
BATCH = 4
SEQ = 8192
DEPTH = 2
DEC_BATCH = 32
DEC_SEQ = 16
PAST_LEN = 1024

CHUNK = 64
HA = 8
DN = 128
DR = 64
DVA = 128
Q_LORA = 512
KV_LORA = 256
Q_BLOCK = 128
HB = 8
DHB = 128
BAND_PREV = 8
REL_CLIP = 128
HR = 8
DKR = 64
DVR = 128
D_FF = 5632
CONV_W = 3
ROPE_THETA = 10000.0
ALPHA = (2 * DEPTH) ** 0.25
BETA = (8 * DEPTH) ** -0.25
EPS = 1e-5
SPLITS = (Q_LORA, KV_LORA, DR, HB * DHB, HB * DHB, HB * DHB,
          HR * DKR, HR * DKR, HR * DVR, HR * DVR, D_MODEL, D_MODEL, D_MODEL)
N_IN = sum(SPLITS)
SPLIT_POINTS = tuple(np.cumsum(SPLITS)[:-1].tolist())

kernel_name = 'hybrid_streaming_encoder_step'


def _layer_norm(x, g, b):
    xf = x.astype(jnp.float32)
    mu = jnp.mean(xf, -1, keepdims=True)
    var = jnp.mean(jnp.square(xf - mu), -1, keepdims=True)
    y = (xf - mu) * lax.rsqrt(var + EPS) * g.astype(jnp.float32) + b.astype(jnp.float32)
    return y.astype(x.dtype)


def _rms_norm(x, g):
    xf = x.astype(jnp.float32)
    y = xf * lax.rsqrt(jnp.mean(xf * xf, -1, keepdims=True) + EPS) * g.astype(jnp.float32)
    return y.astype(x.dtype)


def _rope(x, pos):
    half = x.shape[-1] // 2
    inv = ROPE_THETA ** (-jnp.arange(half, dtype=jnp.float32) / half)
    ang = pos.astype(jnp.float32)[:, None] * inv[None, :]
    shape = (1, x.shape[1]) + (1,) * (x.ndim - 3) + (half,)
    cos, sin = jnp.cos(ang).reshape(shape), jnp.sin(ang).reshape(shape)
    xf = x.astype(jnp.float32)
    x1, x2 = xf[..., :half], xf[..., half:]
    return jnp.concatenate([x1 * cos - x2 * sin, x2 * cos + x1 * sin], -1).astype(x.dtype)


def _mla_attend(q_nope, q_rope, k_nope, k_rope, v, q_pos, k_pos):
    s = (jnp.einsum('bqhd,bkhd->bhqk', q_nope, k_nope)
         + jnp.einsum('bqhr,bkr->bhqk', q_rope, k_rope)).astype(jnp.float32) * (DN + DR) ** -0.5
    allowed = (k_pos[None, :] // CHUNK) <= (q_pos[:, None] // CHUNK)
    s = jnp.where(allowed[None, None], s, -jnp.inf)
    p = jax.nn.softmax(s, axis=-1).astype(v.dtype)
    return jnp.einsum('bhqk,bkhe->bqhe', p, v)


def _mla_block_sweep(q_nope, q_rope, k_nope, k_rope, v, pos):
    B, T = q_nope.shape[:2]
    nb = T // Q_BLOCK

    def blk(a):
        qn, qr, qp = a
        return _mla_attend(qn, qr, k_nope, k_rope, v, qp, pos)

    o = lax.map(blk, (q_nope.reshape(B, nb, Q_BLOCK, HA, DN).swapaxes(0, 1),
                      q_rope.reshape(B, nb, Q_BLOCK, HA, DR).swapaxes(0, 1),
                      pos.reshape(nb, Q_BLOCK)))
    return o.swapaxes(0, 1).reshape(B, T, HA, DVA)


def _band_attend(q, k, v, q_pos, k_pos, rel_bias):
    s = jnp.einsum('bqhd,bkhd->bhqk', q, k).astype(jnp.float32) * DHB ** -0.5
    rel = jnp.clip(q_pos[:, None] - k_pos[None, :], -REL_CLIP, REL_CLIP) + REL_CLIP
    s = s + rel_bias[:, rel].astype(jnp.float32)[None]
    s = jnp.where((k_pos >= 0)[None, None, None, :], s, -jnp.inf)
    p = jax.nn.softmax(s, axis=-1).astype(v.dtype)
    return jnp.einsum('bhqk,bkhe->bqhe', p, v)


def _band_prompt(q, k, v, rel_bias):
    B, T = q.shape[:2]
    nc = T // CHUNK
    pad = BAND_PREV * CHUNK
    kp = jnp.pad(k, ((0, 0), (pad, 0), (0, 0), (0, 0)))
    vp = jnp.pad(v, ((0, 0), (pad, 0), (0, 0), (0, 0)))

    def one(a):
        qc, n = a
        start = n * CHUNK
        kb = lax.dynamic_slice_in_dim(kp, start, pad + CHUNK, axis=1)
        vb = lax.dynamic_slice_in_dim(vp, start, pad + CHUNK, axis=1)
        q_pos = start + jnp.arange(CHUNK, dtype=jnp.int32)
        k_pos = start - pad + jnp.arange(pad + CHUNK, dtype=jnp.int32)
        return _band_attend(qc, kb, vb, q_pos, k_pos, rel_bias)

    o = lax.map(one, (q.reshape(B, nc, CHUNK, HB, DHB).swapaxes(0, 1),
                      jnp.arange(nc, dtype=jnp.int32)))
    return o.swapaxes(0, 1).reshape(B, T, HB, DHB)


def _ret_log_decay():
    return jnp.log1p(-jnp.exp2(-5.0 - jnp.arange(HR, dtype=jnp.float32)))


def _ret_chunk(S, q, k, v, lg):
    L = q.shape[1]
    idx = jnp.arange(L, dtype=jnp.float32)
    diff = idx[:, None] - idx[None, :]
    dmask = jnp.where(diff[None] >= 0, jnp.exp(jnp.maximum(diff, 0.0)[None] * lg[:, None, None]), 0.0)
    inner = jnp.einsum('bnhd,bmhd->bhnm', q, k) * dmask[None]
    o = jnp.einsum('bhnm,bmhe->bnhe', inner, v)
    q_dec = q * jnp.exp((idx[:, None] + 1.0) * lg[None, :])[None, :, :, None]
    o = o + jnp.einsum('bnhd,bhde->bnhe', q_dec, S)
    k_dec = k * jnp.exp((L - 1.0 - idx)[:, None] * lg[None, :])[None, :, :, None]
    S_new = jnp.exp(L * lg)[None, :, None, None] * S + jnp.einsum('bmhd,bmhe->bhde', k_dec, v)
    return S_new, o


def _ret_prompt(q, k, v, lg):
    B, T = q.shape[:2]
    nc = T // CHUNK

    def to_c(a):
        return a.reshape(B, nc, CHUNK, HR, a.shape[-1]).swapaxes(0, 1)

    S0 = jnp.zeros((B, HR, DKR, DVR), jnp.float32)
    S, o = lax.scan(lambda S, a: _ret_chunk(S, a[0], a[1], a[2], lg), S0, (to_c(q), to_c(k), to_c(v)))
    return S, o.swapaxes(0, 1).reshape(B, T, HR, DVR)


def _head_norm(o, g):
    B, T = o.shape[:2]
    mu = jnp.mean(o, -1, keepdims=True)
    var = jnp.mean(jnp.square(o - mu), -1, keepdims=True)
    return ((o - mu) * lax.rsqrt(var + EPS)).reshape(B, T, HR * DVR) * g.astype(jnp.float32)


def _conv_ffn(h, prev, w_a, w_b, cw, cb, w_down):
    T = h.shape[1]
    a = h @ w_a
    b = h @ w_b
    ap = jnp.concatenate([prev.astype(a.dtype), a], axis=1)
    conv = cb + ap[:, 0:T] * cw[0] + ap[:, 1:T + 1] * cw[1] + ap[:, 2:T + 2] * cw[2]
    y = (jax.nn.gelu(conv, approximate=False) * b) @ w_down
    return y, ap[:, T:]


def _layer(x, c, pos, cache, prm):
    (w_ada, b_ada, w_in, g_q, g_kv, w_uq, w_ukv, rel_bias, g_rn, w_pa, w_pb, w_pc,
     w_o, ln1_g, ln1_b, w_fa, w_fb, cw, cb, w_fd, ln2_g, ln2_b) = prm
    B, T, _ = x.shape
    ada = jax.nn.silu(c) @ w_ada + b_ada
    sh1, sc1, gt1, sh2, sc2, gt2 = [a[:, None, :] for a in jnp.split(ada, 6, axis=-1)]
    h = x * (1.0 + sc1) + sh1
    (cq, ckv_raw, kr_raw, qb, kb, vb, qr, kr, vr, gr, g_a, g_b, g_c) = jnp.split(h @ w_in, SPLIT_POINTS, axis=-1)

    qa = (_rms_norm(cq, g_q) @ w_uq).reshape(B, T, HA, DN + DR)
    q_nope, q_rope = qa[..., :DN], _rope(qa[..., DN:], pos)
    ckv = _rms_norm(ckv_raw, g_kv)
    krope = _rope(kr_raw, pos)
    if cache is None:
        ckv_all, kr_all, k_pos = ckv, krope, pos
    else:
        ckv_all = jnp.concatenate([cache[0].astype(ckv.dtype), ckv], axis=1)
        kr_all = jnp.concatenate([cache[1].astype(krope.dtype), krope], axis=1)
        k_pos = jnp.arange(ckv_all.shape[1], dtype=jnp.int32)
    kv = (ckv_all @ w_ukv).reshape(B, ckv_all.shape[1], HA, DN + DVA)
    k_nope, va = kv[..., :DN], kv[..., DN:]
    if cache is None:
        oa = _mla_block_sweep(q_nope, q_rope, k_nope, kr_all, va, pos)
    else:
        oa = _mla_attend(q_nope, q_rope, k_nope, kr_all, va, pos, k_pos)
    oa = oa.reshape(B, T, HA * DVA)

    qb = qb.reshape(B, T, HB, DHB)
    kb = kb.reshape(B, T, HB, DHB)
    vb = vb.reshape(B, T, HB, DHB)
    if cache is None:
        ob = _band_prompt(qb, kb, vb, rel_bias)
        keep = min(BAND_PREV * CHUNK, T)
        bk_new, bv_new = kb[:, T - keep:], vb[:, T - keep:]
    else:
        kc = cache[2].astype(kb.dtype)
        vc = cache[3].astype(vb.dtype)
        nk = kc.shape[1]
        kb_pos = pos[0] - nk + jnp.arange(nk + T, dtype=jnp.int32)
        ob = _band_attend(qb, jnp.concatenate([kc, kb], 1), jnp.concatenate([vc, vb], 1), pos, kb_pos, rel_bias)
        bk_new, bv_new = kb, vb
    ob = ob.reshape(B, T, HB * DHB)

    lg = _ret_log_decay()
    q_r = _rope(qr.reshape(B, T, HR, DKR), pos).astype(jnp.float32) * DKR ** -0.5
    k_r = _rope(kr.reshape(B, T, HR, DKR), pos).astype(jnp.float32)
    v_r = vr.reshape(B, T, HR, DVR).astype(jnp.float32)
    if cache is None:
        S_new, o_r = _ret_prompt(q_r, k_r, v_r, lg)
    else:
        S_new, o_r = _ret_chunk(cache[4].astype(jnp.float32), q_r, k_r, v_r, lg)
    oc = (jax.nn.silu(gr.astype(jnp.float32)) * _head_norm(o_r, g_rn)).astype(x.dtype)

    merged = (jax.nn.sigmoid(g_a) * (oa @ w_pa) + jax.nn.sigmoid(g_b) * (ob @ w_pb)
              + jax.nn.sigmoid(g_c) * (oc @ w_pc))
    x = _layer_norm(ALPHA * x + (1.0 + gt1) * (merged @ w_o), ln1_g, ln1_b)

    h2 = x * (1.0 + sc2) + sh2
    prev = jnp.zeros((B, CONV_W - 1, D_FF), x.dtype) if cache is None else cache[5]
    y, conv_new = _conv_ffn(h2, prev, w_fa, w_fb, cw, cb, w_fd)
    x = _layer_norm(ALPHA * x + (1.0 + gt2) * y, ln2_g, ln2_b)
    return x, (ckv, krope, bk_new, bv_new, S_new, conv_new)


def setup_inputs(seed: int = 0) -> dict:
    key = jax.random.key(seed)
    ks = iter(jax.random.split(key, 48))
    f32 = jnp.float32

    def nrm(shape, scale):
        return jax.random.normal(next(ks), shape, f32) * scale

    band_cache = min(BAND_PREV * CHUNK, PAST_LEN)
    L = DEPTH
    return {
        'x_prompt': nrm((BATCH, SEQ, D_MODEL), 1.0),
        'x_sample': nrm((DEC_BATCH, DEC_SEQ, D_MODEL), 1.0),
        'c_prompt': nrm((BATCH, D_MODEL), 1.0),
        'c_sample': nrm((DEC_BATCH, D_MODEL), 1.0),
        'cache_mla_ckv': nrm((L, DEC_BATCH, PAST_LEN, KV_LORA), 1.0),
        'cache_mla_krope': nrm((L, DEC_BATCH, PAST_LEN, DR), 1.0),
        'cache_band_k': nrm((L, DEC_BATCH, band_cache, HB, DHB), 1.0),
        'cache_band_v': nrm((L, DEC_BATCH, band_cache, HB, DHB), 1.0),
        'state_ret': nrm((L, DEC_BATCH, HR, DKR, DVR), 1.0),
        'state_conv': nrm((L, DEC_BATCH, CONV_W - 1, D_FF), 1.0),
        'w_ada': nrm((L, D_MODEL, 6 * D_MODEL), 0.1 * D_MODEL ** -0.5),
        'b_ada': nrm((L, 6 * D_MODEL), 0.01),
        'w_in': nrm((L, D_MODEL, N_IN), D_MODEL ** -0.5),
        'g_q_lora': 1.0 + nrm((L, Q_LORA), 0.1),
        'g_kv_lora': 1.0 + nrm((L, KV_LORA), 0.1),
        'w_uq': nrm((L, Q_LORA, HA * (DN + DR)), Q_LORA ** -0.5),
        'w_ukv': nrm((L, KV_LORA, HA * (DN + DVA)), KV_LORA ** -0.5),
        'rel_bias': nrm((L, HB, 2 * REL_CLIP + 1), 0.2),
        'g_ret_norm': 1.0 + nrm((L, HR * DVR), 0.1),
        'w_branch_a': nrm((L, HA * DVA, D_MODEL), (HA * DVA) ** -0.5),
        'w_branch_b': nrm((L, HB * DHB, D_MODEL), (HB * DHB) ** -0.5),
        'w_branch_c': nrm((L, HR * DVR, D_MODEL), (HR * DVR) ** -0.5),
        'w_o': nrm((L, D_MODEL, D_MODEL), BETA * D_MODEL ** -0.5),
        'ln1_g': 1.0 + nrm((L, D_MODEL), 0.1),
        'ln1_b': nrm((L, D_MODEL), 0.02),
        'w_ff_a': nrm((L, D_MODEL, D_FF), D_MODEL ** -0.5),
        'w_ff_b': nrm((L, D_MODEL, D_FF), D_MODEL ** -0.5),
        'conv_w': nrm((L, CONV_W, D_FF), CONV_W ** -0.5),
        'conv_b': nrm((L, D_FF), 0.02),
        'w_ff_down': nrm((L, D_FF, D_MODEL), BETA * D_FF ** -0.5),
        'ln2_g': 1.0 + nrm((L, D_MODEL), 0.1),
        'ln2_b': nrm((L, D_MODEL), 0.02),
    }


def reference(x_prompt, x_sample, c_prompt, c_sample, cache_mla_ckv, cache_mla_krope,
              cache_band_k, cache_band_v, state_ret, state_conv, w_ada, b_ada, w_in,
              g_q_lora, g_kv_lora, w_uq, w_ukv, rel_bias, g_ret_norm, w_branch_a,
              w_branch_b, w_branch_c, w_o, ln1_g, ln1_b, w_ff_a, w_ff_b, conv_w, conv_b,
              w_ff_down, ln2_g, ln2_b):
    past = cache_mla_ckv.shape[2]
    pos_p = jnp.arange(x_prompt.shape[1], dtype=jnp.int32)
    pos_s = past + jnp.arange(x_sample.shape[1], dtype=jnp.int32)
    y_prompt, y_sample = x_prompt, x_sample
    new_p, new_s = [], []
    for l in range(DEPTH):
        prm = (w_ada[l], b_ada[l], w_in[l], g_q_lora[l], g_kv_lora[l], w_uq[l], w_ukv[l],
               rel_bias[l], g_ret_norm[l], w_branch_a[l], w_branch_b[l], w_branch_c[l],
               w_o[l], ln1_g[l], ln1_b[l], w_ff_a[l], w_ff_b[l], conv_w[l], conv_b[l],
               w_ff_down[l], ln2_g[l], ln2_b[l])
        y_prompt, st_p = _layer(y_prompt, c_prompt, pos_p, None, prm)
        cache_l = (cache_mla_ckv[l], cache_mla_krope[l], cache_band_k[l], cache_band_v[l],
                   state_ret[l], state_conv[l])
        y_sample, st_s = _layer(y_sample, c_sample, pos_s, cache_l, prm)
        new_p.append(st_p)
        new_s.append(st_s)
    mla_ckv_p, mla_kr_p, band_k_p, band_v_p, ret_p, conv_p = [jnp.stack(z) for z in zip(*new_p)]
    mla_ckv_s, mla_kr_s, band_k_s, band_v_s, ret_s, conv_s = [jnp.stack(z) for z in zip(*new_s)]
    return (y_prompt, y_sample, mla_ckv_p, mla_kr_p, band_k_p, band_v_p, ret_p, conv_p,
            mla_ckv_s, mla_kr_s, band_k_s, band_v_s, ret_s, conv_s)
```

```python
from contextlib import ExitStack
import numpy as np
import concourse.bass as bass
import concourse.mybir as mybir
from concourse.bass_utils import run_bass_kernel_spmd

F32 = mybir.dt.float32
BF16 = mybir.dt.bfloat16
AF = mybir.ActivationFunctionType
ALU = mybir.AluOpType
AX = mybir.AxisListType

D = 2048
KC = 16
NIN = 13120
DFF = 5632
FC = 44
ALPHA = 4 ** 0.25
EPS = 1e-5
O_CQ, O_CKV, O_KR, O_QB, O_KB, O_VB, O_QR, O_KRR, O_VR, O_GR, O_GA, O_GB, O_GC = (
    0, 512, 768, 832, 1856, 2880, 3904, 4416, 4928, 5952, 6976, 9024, 11072)
SC_MLA = 192 ** -0.5
SC_B = 128 ** -0.5
BIAS_DS = (128, 0, -128, -256, -384)


class Buf:
    def __init__(self, t=None):
        self.t = t
        self.w = {}
        self.r = {}


class Ring:
    def __init__(self, bufs):
        self.bufs, self.i = bufs, 0

    @property
    def t(self):
        return self.bufs[self.i].t

    @property
    def cur(self):
        return self.bufs[self.i]

    def adv(self):
        self.i = (self.i + 1) % len(self.bufs)


def _res(bs):
    return [b.cur if isinstance(b, Ring) else b for b in bs]


class Eng:
    def __init__(self, name, h, sem):
        self.name, self.h, self.sem, self.cnt, self.seen = name, h, sem, 0, {}


class Slot:
    def __init__(self, sem, key):
        self.sem, self.val, self.key = sem, 0, key


class KB:
    def __init__(self, nc, es):
        self.nc, self.es = nc, es
        self.E = {}
        for name, h in (("pe", nc.tensor), ("act", nc.scalar), ("dve", nc.vector),
                        ("pool", nc.gpsimd), ("sp", nc.sync)):
            self.E[name] = Eng(name, h, es.enter_context(nc.semaphore("s_" + name)))
        self.slots = {q: [Slot(es.enter_context(nc.semaphore(f"d_{q}{i}")), f"d_{q}{i}") for i in range(10)]
                      for q in ("sp", "pool", "pe", "act")}
        self.slot_i = {"sp": 0, "pool": 0, "pe": 0, "act": 0}
        self.nt = 0

    def sb(self, shape, dt, name=None):
        self.nt += 1
        return Buf(self.es.enter_context(self.nc.sbuf_tensor("sb_" + (name or f"t{self.nt}"), list(shape), dt)))

    def ps(self, shape, dt, name=None):
        self.nt += 1
        return Buf(self.es.enter_context(self.nc.psum_tensor("ps_" + (name or f"p{self.nt}"), list(shape), dt)))

    def _wait(self, e, R, W, is_dma=False):
        R, W = _res(R), _res(W)
        deps = {}
        for b in R:
            for k, v in b.w.items():
                if deps.get(k, (None, 0))[1] < v[1]:
                    deps[k] = v
        for b in W:
            for dd in (b.w, b.r):
                for k, v in dd.items():
                    if deps.get(k, (None, 0))[1] < v[1]:
                        deps[k] = v
        for k, (sem, val) in deps.items():
            if e.name == "pe" and k == "s_pe" and not is_dma:
                continue
            if e.seen.get(k, 0) < val:
                e.h.wait_ge(sem, val)
                e.seen[k] = val

    def _mark(self, key, tok, R, W):
        R, W = _res(R), _res(W)
        for b in W:
            b.w[key] = tok
            b.r = {}
        for b in R:
            if b not in W:
                b.r[key] = tok

    def op(self, en, fn, R=(), W=()):
        e = self.E[en]
        self._wait(e, R, W)
        ins = fn(e.h)
        e.cnt += 1
        ins.then_inc(e.sem, 1)
        self._mark("s_" + en, (e.sem, e.cnt), R, W)

    def dma(self, q, out, in_, R=(), W=()):
        e = self.E[q]
        sl = self.slots[q][self.slot_i[q]]
        self.slot_i[q] = (self.slot_i[q] + 1) % len(self.slots[q])
        key = sl.key
        if sl.val and e.seen.get(key, 0) < sl.val:
            e.h.wait_ge(sl.sem, sl.val)
            e.seen[key] = sl.val
        self._wait(e, R, W, is_dma=True)
        ins = e.h.dma_start(out=out, in_=in_)
        sl.val += 16
        ins.then_inc(sl.sem, 16)
        self._mark(key, (sl.sem, sl.val), R, W)


def build(TP, NSB, NL, PAST=1024, BPAST=512, SEG=256):
    nc = bass.Bass("TRN2", target_bir_lowering=False)
    es = ExitStack()
    K = KB(nc, es)
    NSEQ = 1 + NSB
    KEEP = min(512, TP)
    TS = 16

    def din(name, shape):
        return nc.dram_tensor(name, list(shape), F32, kind="ExternalInput")

    def dout(name, shape):
        return nc.dram_tensor(name, list(shape), F32, kind="ExternalOutput")

    def dscr(name, shape, dt=BF16):
        return Buf(nc.dram_tensor(name, list(shape), dt))

    I = {}
    I["xT_p"] = din("xT_p", [D, TP])
    I["xT_s"] = din("xT_s", [NSB, D, TS])
    I["cT"] = din("cT", [128, KC, NSEQ])
    I["ckvT_c"] = din("ckvT_c", [NL, NSB, 256, PAST])
    I["krT_c"] = din("krT_c", [NL, NSB, 64, PAST])
    I["bkT_c"] = din("bkT_c", [NL, NSB, 8, 128, BPAST])
    I["bv_c"] = din("bv_c", [NL, NSB, BPAST, 1024])
    I["sret"] = din("sret", [NL, NSB, 64, 8, 128])
    I["sconvT"] = din("sconvT", [NL, NSB, 128, FC, 2])
    I["w_ada"] = din("w_ada", [NL, D, 6 * D])
    I["b_adaT"] = din("b_adaT", [NL, 128, 96])
    I["w_in"] = din("w_in", [NL, D, NIN])
    I["g_q"] = din("g_q", [NL, 128, 4])
    I["g_kv"] = din("g_kv", [NL, 128, 2])
    I["w_uq"] = din("w_uq", [NL, 512, 1536])
    I["w_ukv"] = din("w_ukv", [NL, 256, 2048])
    I["btoe"] = din("btoe", [NL, 8, len(BIAS_DS), 128, 512])
    I["rbb"] = din("rbb", [NL, 8, 128, 257])
    I["g_rn"] = din("g_rn", [NL, 128, 8])
    I["w_pa"] = din("w_pa", [NL, 1024, D])
    I["w_pb"] = din("w_pb", [NL, 1024, D])
    I["w_pc"] = din("w_pc", [NL, 1024, D])
    I["w_o"] = din("w_o", [NL, D, D])
    I["ln1"] = din("ln1", [NL, 128, 2, KC])
    I["w_fa"] = din("w_fa", [NL, D, DFF])
    I["w_fb"] = din("w_fb", [NL, D, DFF])
    I["cwT"] = din("cwT", [NL, 128, FC, 4])
    I["w_fd"] = din("w_fd", [NL, DFF, D])
    I["ln2"] = din("ln2", [NL, 128, 2, KC])
    I["ropeP"] = din("ropeP", [2, 64, TP])
    I["ropeS"] = din("ropeS", [2, 64, TS])
    I["ident"] = din("ident", [128, 128])
    I["dm"] = din("dm", [8, 128, 128])
    I["dq"] = din("dq", [64, 8, 128])
    I["dk"] = din("dk", [2, 128, 8])

    O = {}
    O["yT_p"] = dout("yT_p", [D, TP])
    O["yT_s"] = dout("yT_s", [NSB, D, TS])
    O["ckvT_p"] = dout("ckvT_p", [NL, 256, TP])
    O["krT_p"] = dout("krT_p", [NL, 64, TP])
    O["bkT_p"] = dout("bkT_p", [NL, 8, 128, KEEP])
    O["bv_p"] = dout("bv_p", [NL, KEEP, 1024])
    O["ret_p"] = dout("ret_p", [NL, 64, 8, 128])
    O["convT_p"] = dout("convT_p", [NL, 128, FC, 2])
    O["ckvT_s"] = dout("ckvT_s", [NL, NSB, 256, TS])
    O["krT_s"] = dout("krT_s", [NL, NSB, 64, TS])
    O["bkT_s"] = dout("bkT_s", [NL, NSB, 8, 128, TS])
    O["bv_s"] = dout("bv_s", [NL, NSB, TS, 1024])
    O["ret_s"] = dout("ret_s", [NL, NSB, 64, 8, 128])
    O["convT_s"] = dout("convT_s", [NL, NSB, 128, FC, 2])
    OB = {k: Buf(v) for k, v in O.items()}

    seqs = []
    for s in range(NSEQ):
        if s == 0:
            sq = dict(T=TP, past=0, bpast=0, seg=min(SEG, TP))
        else:
            sq = dict(T=TS, past=PAST, bpast=BPAST, seg=TS)
        sq["TK"] = sq["past"] + sq["T"]
        sq["TB"] = sq["bpast"] + sq["T"]
        sq["knT"] = dscr(f"knT{s}", [8, 128, sq["TK"]])
        sq["krT"] = dscr(f"krT{s}", [64, sq["TK"]])
        sq["vm"] = dscr(f"vm{s}", [sq["TK"], 1024])
        sq["kbT"] = dscr(f"kbT{s}", [8, 128, sq["TB"]])
        sq["vb"] = dscr(f"vb{s}", [sq["TB"], 1024])
        sq["x1"] = dscr(f"x1_{s}", [D, sq["T"]], F32)
        seqs.append(sq)
    gates = dscr("gates", [48, 128, 512])
    qscr = dscr("qscr", [3, 8, 128, 512])

    sb, ps = K.sb, K.ps
    xs = sb([128, KC, SEG], F32, "xs")
    hT = sb([128, KC, SEG], BF16, "hT")
    wst = [sb([128, KC, 128], F32, f"wst{i}") for i in range(1)]
    wbf = [sb([128, KC, 128], BF16, f"wbf{i}") for i in range(3)]
    wrotc = [sb([128, KC, 128], BF16, f"wrotc{i}") for i in range(1)]
    wst4 = sb([128, 2, 512], F32, "wst4")
    wbf4 = [sb([128, KC, 512], BF16, f"wbf4_{i}") for i in range(1)]
    cq = sb([128, 4, SEG], F32, "cq")
    ckvr = sb([128, 2, SEG], F32, "ckvr")
    cqn = sb([128, 4, SEG], BF16, "cqn")
    ckvb = sb([128, 2, SEG], BF16, "ckvb")
    qret = sb([64, 8, SEG], BF16, "qret")
    kret = sb([64, 8, SEG], BF16, "kret")
    vr = sb([128, SEG // 128, 1024], BF16, "vr")
    sgr = sb([128, 8, SEG], BF16, "sgr")
    arena = Buf(es.enter_context(nc.sbuf_tensor("arena", [128, 25600], BF16)))
    oabc = sb([128, 24, SEG], BF16, "oabc")
    merged = sb([128, KC, SEG], BF16, "merged")
    acc = sb([128, SEG], F32, "acc")
    tmpf = Ring([sb([128, 512], F32, f"tmpf{i}") for i in range(2)])
    tmpf2 = Ring([sb([128, SEG], F32, f"tmpf2_{i}") for i in range(2)])
    tmpb = Ring([sb([128, 512], BF16, f"tmpb{i}") for i in range(4)])
    mu = sb([128, SEG], F32, "mu")
    rstd = sb([128, SEG], F32, "rstd")
    PT = [sb([128, SEG], BF16, f"PT{i}") for i in range(2)]
    otok = sb([128, 128], BF16, "otok")
    small = sb([128, 64], F32, "small")
    qst = sb([128, 8, 2], F32, "qst")
    kst = sb([128, NSEQ, 8, 2], F32, "kst")
    ada = sb([128, NSEQ, 96], F32, "ada")
    scb = sb([128, KC, NSEQ], BF16, "scb")
    cTf = sb([128, KC, NSEQ], F32, "cTf")
    badaT = sb([128, 96], F32, "badaT")
    gq = sb([128, 4], F32, "gq")
    gkv = sb([128, 2], F32, "gkv")
    grn = sb([128, 8], F32, "grn")
    ln1 = sb([128, 2, KC], F32, "ln1")
    ln2 = sb([128, 2, KC], F32, "ln2")
    cw = sb([128, FC, 4], F32, "cw")
    ropeC = sb([64, SEG], F32, "ropeC")
    ropeS_ = sb([64, SEG], F32, "ropeS")
    identf = sb([128, 128], F32, "identf")
    identb = sb([128, 128], BF16, "identb")
    onesb = sb([128, 128], BF16, "onesb")
    dm = sb([128, 8, 128], F32, "dm")
    dq = sb([64, 8, 128], F32, "dq")
    dk = sb([128, 2, 8], F32, "dk")
    Sst = sb([64, 8, 128], F32, "Sst")
    Sbf = sb([64, 8, 128], BF16, "Sbf")
    HS = sb([128, FC, 2], F32, "HS")
    ah = Ring([sb([128, SEG + 2], F32, f"ah{i}") for i in range(2)])
    btile = sb([128, SEG], F32, "btile")
    rbt = sb([128, 257], F32, "rbt")
    atm = sb([128, 128], BF16, "atm")
    qd = sb([64, 128], BF16, "qd")
    kdec = sb([128, 64], BF16, "kdec")
    onb = sb([128, 128], BF16, "onb")
    A = arena.t
    OFF_KN, OFF_KR, OFF_V = 0, 8192, 16384

    def kn_v(n0, n1):
        return A[:, OFF_KN + n0:OFF_KN + n1]

    def kr_v(n0, n1):
        return A[0:64, OFF_KR + n0:OFF_KR + n1]

    def v_v(kt, nk, d1=129):
        return A[0:nk, OFF_V + kt * 129:OFF_V + kt * 129 + d1]

    def u_v(j, n):
        return A[:, j * SEG:j * SEG + n]

    pg = [ps([128, 512], F32, f"pg{i}") for i in range(2)]
    pss = [ps([128, 512], F32, f"pss{i}") for i in range(2)]
    po = [ps([128, 512], F32, f"po{i}") for i in range(2)]
    pst = ps([128, 512], F32, "pst")
    ptr = ps([128, 512], BF16, "ptr")
    cnt = {"g": 0, "cv": 0, "w": 0, "s": 0, "p": 0, "wb": 0, "wr": 0}

    def nxt(key, n=2):
        cnt[key] += 1
        return cnt[key] % n

    def cvt_eng():
        cnt["cv"] += 1
        return "dve" if cnt["cv"] % 2 else "act"

    def copy(en, out, in_, R, W):
        if en == "act":
            K.op("act", lambda e: e.copy(out, in_), R, W)
        else:
            K.op(en, lambda e: e.tensor_copy(out, in_), R, W)

    wcache = {}
    lq = {"i": 0}
    WQ = ("sp",)

    def ldq():
        lq["i"] += 1
        return WQ[lq["i"] % len(WQ)]

    def stage_w(Wd, l, k0, nkc, pieces, rot=False, cache=True):
        key = (Wd.name, l, k0, nkc, tuple(pieces), rot)
        c = sum(w_ for _, w_ in pieces)
        bf = wbf[nxt("wb", 3)]
        if cache and key in wcache:
            sc, scr = wcache[key]
            K.dma(ldq(), bf.t[:, 0:nkc, 0:c], sc.t[:, :, :], R=[sc], W=[bf])
            rb_ = None
            if rot:
                rb_ = wrotc[0]
                K.dma(ldq(), rb_.t[:, 0:nkc, 0:c], scr.t[:, :, :], R=[scr], W=[rb_])
            return bf, rb_, c
        st = wst[0]
        Wl = Wd[l].rearrange("(kc p) n -> p kc n", p=128)
        c = 0
        for (c0, wd) in pieces:
            K.dma("sp", st.t[:, 0:nkc, c:c + wd], Wl[:, k0:k0 + nkc, c0:c0 + wd], R=[], W=[st])
            c += wd
        copy(cvt_eng(), bf.t[:, 0:nkc, 0:c], st.t[:, 0:nkc, 0:c], [st], [bf])
        rb_ = None
        if rot:
            rb_ = wrotc[0]
            g = c // 64
            sv = st.t[:, 0:nkc, 0:c].rearrange("p k (g t f) -> p k g t f", g=g, t=2)
            rv = rb_.t[:, 0:nkc, 0:c].rearrange("p k (g t f) -> p k g t f", g=g, t=2)
            K.op("act", lambda e: e.mul(rv[:, :, :, 0, :], sv[:, :, :, 1, :], -1.0), [st], [rb_])
            K.op("dve", lambda e: e.tensor_copy(rv[:, :, :, 1, :], sv[:, :, :, 0, :]), [st], [rb_])
        if cache:
            nm = f"wc{len(wcache)}"
            sc = dscr(nm, [128, nkc, c])
            K.dma("pool", sc.t[:, :, :], bf.t[:, 0:nkc, 0:c], R=[bf], W=[sc])
            scr = None
            if rot:
                scr = dscr(nm + "r", [128, nkc, c])
                K.dma("pool", scr.t[:, :, :], rb_.t[:, 0:nkc, 0:c], R=[rb_], W=[scr])
            wcache[key] = (sc, scr)
        return bf, rb_, c

    def mm_fm(pt, m, wb, nkc, rhs_fn, n, R, first=True, last=True):
        def f(e):
            ins = None
            for kc in range(nkc):
                ins = e.matmul(pt.t[0:m, 0:n], wb.t[:, kc, 0:m], rhs_fn(kc),
                               start=(first and kc == 0), stop=(last and kc == nkc - 1))
            return ins
        K.op("pe", f, [wb] + list(R), [pt])

    def gemm_fm(Wd, l, Kdim, pieces, rhs_buf, rhs_fn, n, rot=False, cache=True):
        nk = Kdim // 128
        pt = pg[nxt("g")]
        pr = None
        k0 = 0
        while k0 < nk:
            nkc = min(KC, nk - k0)
            wb, wr_, m = stage_w(Wd, l, k0, nkc, pieces, rot, cache)
            mm_fm(pt, m, wb, nkc, lambda kc, k0=k0: rhs_fn(k0 + kc), n, [rhs_buf],
                  first=(k0 == 0), last=(k0 + nkc == nk))
            if rot:
                pr = pss[nxt("s")]
                mm_fm(pr, m, wr_, nkc, lambda kc, k0=k0: rhs_fn(k0 + kc), n, [rhs_buf])
            k0 += nkc
        return pt, pr, m

    def gemm_tm(Wd, l, Kdim, pieces, lhs_buf, lhs_fn, ntok, epi):
        nk = Kdim // 128
        Wl = Wd[l].rearrange("(kc p) n -> p kc n", p=128)
        wb = wbf4[0]
        c = sum(w_ for _, w_ in pieces)
        key = ("tm", Wd.name, l, tuple(pieces))
        if key in wcache:
            K.dma(ldq(), wb.t[:, 0:nk, 0:c], wcache[key].t[:, :, :], R=[wcache[key]], W=[wb])
        else:
            for k0 in range(0, nk, 2):
                c = 0
                for (c0, wd) in pieces:
                    K.dma("sp", wst4.t[:, 0:2, c:c + wd], Wl[:, k0:k0 + 2, c0:c0 + wd], R=[], W=[wst4])
                    c += wd
                copy(cvt_eng(), wb.t[:, k0:k0 + 2, 0:c], wst4.t[:, 0:2, 0:c], [wst4], [wb])
            sc = dscr(f"wc{len(wcache)}", [128, nk, c])
            K.dma("pool", sc.t[:, :, :], wb.t[:, 0:nk, 0:c], R=[wb], W=[sc])
            wcache[key] = sc
        for tb in range((ntok + 127) // 128):
            nt = min(128, ntok - tb * 128)
            pt = pg[nxt("g")]

            def f(e, tb=tb, nt=nt, pt=pt):
                ins = None
                for kc in range(nk):
                    ins = e.matmul(pt.t[0:nt, 0:c], lhs_fn(kc, tb * 128, nt), wb.t[:, kc, 0:c],
                                   start=(kc == 0), stop=(kc == nk - 1))
                return ins
            K.op("pe", f, [wb, lhs_buf], [pt])
            epi(pt, tb, nt, c)

    def ones_sum(pt, srcs, n, Rb):
        def f(e):
            ins = None
            for i, (ap, k) in enumerate(srcs):
                ins = e.matmul(pt.t[:, 0:n], onesb.t[0:k, :], ap, start=(i == 0), stop=(i == len(srcs) - 1))
            return ins
        K.op("pe", f, [onesb] + list(Rb), [pt])

    def rsqrt_inplace(buf, ap, scale, eps):
        K.op("dve", lambda e: e.tensor_scalar(ap, ap, scale, eps, ALU.mult, ALU.add), [buf], [buf])
        K.op("act", lambda e: e.activation(ap, ap, AF.Sqrt), [buf], [buf])
        K.op("dve", lambda e: e.reciprocal(ap, ap), [buf], [buf])

    K.dma("sp", identf.t[:], I["ident"].ap(), W=[identf])
    copy("dve", identb.t[:], identf.t[:], [identf], [identb])
    K.op("dve", lambda e: e.memset(onesb.t[:], 1.0), [], [onesb])
    K.dma("sp", dm.t[:], I["dm"].ap().rearrange("h m n -> m h n"), W=[dm])
    K.dma("sp", dq.t[:], I["dq"].ap(), W=[dq])
    K.dma("sp", dk.t[:], I["dk"].ap().rearrange("t m h -> m t h"), W=[dk])
    K.dma("sp", cTf.t[:], I["cT"].ap(), W=[cTf])
    K.op("act", lambda e: e.activation(scb.t[:], cTf.t[:], AF.Silu), [cTf], [scb])

    def layer(l):
        last_layer = (l == NL - 1)
        for (tb, nm) in ((badaT, "b_adaT"), (gq, "g_q"), (gkv, "g_kv"), (grn, "g_rn"),
                         (ln1, "ln1"), (ln2, "ln2"), (cw, "cwT")):
            K.dma("sp", tb.t[:], I[nm][l], W=[tb])
        for j in range(96):
            pt, _, m = gemm_fm(I["w_ada"], l, D, [(j * 128, 128)], scb, lambda kc: scb.t[:, kc, :], NSEQ, cache=False)
            K.op("dve", lambda e, pt=pt, j=j: e.tensor_scalar(ada.t[:, :, j], pt.t[:, 0:NSEQ],
                                                              badaT.t[:, j:j + 1], None, ALU.add), [pt, badaT], [ada])
        for base in (16, 32, 64, 80):
            K.op("dve", lambda e, base=base: e.tensor_scalar_add(ada.t[:, :, base:base + 16],
                                                                 ada.t[:, :, base:base + 16], 1.0), [ada], [ada])
        for s in range(NSEQ):
            sequence(l, s)

    def adav(s, base, j):
        return ada.t[:, s, base + j:base + j + 1]

    def sequence(l, s):
        sq = seqs[s]
        last_layer = (l == NL - 1)
        T, past, bpast = sq["T"], sq["past"], sq["bpast"]
        isP = (s == 0)
        b = s - 1
        if l == 0:
            xsrc = (I["xT_p"].ap() if isP else I["xT_s"][b])
            xsrc_buf = Buf()
        else:
            xsrc, xsrc_buf = sq["x1"].t.ap(), sq["x1"]
        if last_layer:
            xdst = (O["yT_p"].ap() if isP else O["yT_s"][b])
            xdst_buf = OB["yT_p"] if isP else OB["yT_s"]
        else:
            xdst, xdst_buf = sq["x1"].t.ap(), sq["x1"]
        knT, krT, vm, kbT, vbd = sq["knT"], sq["krT"], sq["vm"], sq["kbT"], sq["vb"]
        if isP:
            K.op("dve", lambda e: e.memset(Sst.t[:], 0.0), [], [Sst])
            K.op("dve", lambda e: e.memset(HS.t[:], 0.0), [], [HS])
            K.op("dve", lambda e: e.memset(kst.t[:, s], 0.0), [], [kst])
        else:
            K.dma("sp", Sst.t[:], I["sret"][l, b], W=[Sst])
            K.dma("sp", HS.t[:], I["sconvT"][l, b], W=[HS])
            K.op("dve", lambda e: e.memset(kst.t[:, s], 0.0), [], [kst])
            for t0 in range(0, past, SEG):
                K.dma("sp", ckvr.t[:, :, :], I["ckvT_c"][l, b].rearrange("(kc p) t -> p kc t", p=128)[:, :, t0:t0 + SEG],
                      W=[ckvr])
                copy("dve", ckvb.t[:], ckvr.t[:], [ckvr], [ckvb])
                tmpf.adv()
                K.dma("sp", tmpf.t[0:64, 0:SEG], I["krT_c"][l, b][:, t0:t0 + SEG], W=[tmpf])
                tmpb.adv()
                copy("dve", tmpb.t[0:64, 0:SEG], tmpf.t[0:64, 0:SEG], [tmpf], [tmpb])
                K.dma("pool", krT.t[:, t0:t0 + SEG], tmpb.t[0:64, 0:SEG], R=[tmpb], W=[krT])
                kv_up(l, s, t0, SEG, None)
            for h in range(8):
                tmpf.adv()
                K.dma("sp", tmpf.t[:, 0:bpast], I["bkT_c"][l, b, h], W=[tmpf])
                tmpb.adv()
                copy("dve", tmpb.t[:, 0:bpast], tmpf.t[:, 0:bpast], [tmpf], [tmpb])
                K.dma("pool", kbT.t[h, :, 0:bpast], tmpb.t[:, 0:bpast], R=[tmpb], W=[kbT])
                kstat(s, h, 1, [(tmpb.t[:, 0:bpast], 128)], bpast, [tmpb])
            for tb in range(bpast // 128):
                for hh in range(2):
                    tmpf.adv()
                    K.dma("sp", tmpf.t[:, :], I["bv_c"][l, b, tb * 128:(tb + 1) * 128, hh * 512:(hh + 1) * 512], W=[tmpf])
                    tmpb.adv()
                    copy("dve", tmpb.t[:], tmpf.t[:], [tmpf], [tmpb])
                    K.dma("pool", vbd.t[tb * 128:(tb + 1) * 128, hh * 512:(hh + 1) * 512], tmpb.t[:], R=[tmpb], W=[vbd])
        nseg = (T + sq["seg"] - 1) // sq["seg"]
        for sg in range(nseg):
            segment(l, s, sg * sq["seg"], min(sq["seg"], T - sg * sq["seg"]), xsrc, xsrc_buf, xdst, xdst_buf)
        if isP:
            K.dma("pool", O["ret_p"][l], Sst.t[:], R=[Sst], W=[OB["ret_p"]])
            K.dma("pool", O["convT_p"][l], HS.t[:], R=[HS], W=[OB["convT_p"]])
        else:
            K.dma("pool", O["ret_s"][l, b], Sst.t[:], R=[Sst], W=[OB["ret_s"]])
            K.dma("pool", O["convT_s"][l, b], HS.t[:], R=[HS], W=[OB["convT_s"]])

    def kstat(s, h, which, srcs, n, Rb):
        pt = pst
        sqs = []
        for i, (ap, k) in enumerate(srcs):
            dst = tmpb if i == 0 else onb
            dv = dst.t[0:k, 0:n]
            K.op("act", lambda e, dv=dv, ap=ap: e.activation(dv, ap, AF.Square), list(Rb), [dst])
            sqs.append((dv, k, dst))
        ones_sum(pt, [(a, k) for a, k, _ in sqs], n, [d for _, _, d in sqs])
        K.op("dve", lambda e: e.reduce_max(small.t[:, 0:1], pt.t[:, 0:n], AX.X), [pt], [small])
        K.op("dve", lambda e: e.tensor_max(kst.t[:, s, h, which:which + 1], kst.t[:, s, h, which:which + 1],
                                           small.t[:, 0:1]), [small, kst], [kst])

    def kv_up(l, s, t0, n, _):
        sq = seqs[s]
        K.dma("sp", merged.t[0:64, 0, 0:n], sq["krT"].t[:, t0:t0 + n], R=[sq["krT"]], W=[merged])
        K.op("act", lambda e: e.activation(merged.t[0:64, 2, 0:n], merged.t[0:64, 0, 0:n], AF.Square), [merged], [merged])
        for h in range(8):
            pt, _, m = gemm_fm(I["w_ukv"], l, 256, [(h * 256, 128)], ckvb, lambda kc: ckvb.t[:, kc, 0:n], n)
            tmpb.adv()
            copy("act", tmpb.t[:, 0:n], pt.t[:, 0:n], [pt], [tmpb])
            K.dma("pool", sq["knT"].t[h, :, t0:t0 + n], tmpb.t[:, 0:n], R=[tmpb], W=[sq["knT"]])
            kstat2(s, h, n)
        for half in range(2):
            def epi(pt, tb, nt, c, half=half):
                tmpb.adv()
                copy("act", tmpb.t[0:nt, 0:c], pt.t[0:nt, 0:c], [pt], [tmpb])
                K.dma("pool", sq["vm"].t[t0 + tb * 128:t0 + tb * 128 + nt, half * 512:half * 512 + 512],
                      tmpb.t[0:nt, 0:c], R=[tmpb], W=[sq["vm"]])
            gemm_tm(I["w_ukv"], l, 256, [((half * 4 + hh) * 256 + 128, 128) for hh in range(4)], ckvb,
                    lambda kc, c0, nt: ckvb.t[:, kc, c0:c0 + nt], n, epi)

    def kstat2(s, h, n):
        K.op("act", lambda e: e.activation(merged.t[:, 1, 0:n], tmpb.t[:, 0:n], AF.Square), [tmpb], [merged])
        ones_sum(pst, [(merged.t[:, 1, 0:n], 128), (merged.t[0:64, 2, 0:n], 64)], n, [merged])
        K.op("dve", lambda e: e.reduce_max(small.t[:, 0:1], pst.t[:, 0:n], AX.X), [pst], [small])
        K.op("dve", lambda e: e.tensor_max(kst.t[:, s, h, 0:1], kst.t[:, s, h, 0:1], small.t[:, 0:1]),
             [small, kst], [kst])

    def rope_epi(pt, pr, m, n, t0, out_ap, out_buf, scale=None):
        tmpf.adv()
        tmpf2.adv()
        K.op("dve", lambda e: e.tensor_tensor(tmpf.t[0:m, 0:n], pt.t[0:m, 0:n], ropeC.t[0:m, 0:n], ALU.mult),
             [pt, ropeC], [tmpf])
        K.op("dve", lambda e: e.tensor_tensor(tmpf2.t[0:m, 0:n], pr.t[0:m, 0:n], ropeS_.t[0:m, 0:n], ALU.mult),
             [pr, ropeS_], [tmpf2])
        if scale is None:
            K.op("dve", lambda e: e.tensor_tensor(out_ap, tmpf.t[0:m, 0:n], tmpf2.t[0:m, 0:n], ALU.add),
                 [tmpf, tmpf2], [out_buf])
        else:
            K.op("dve", lambda e: e.tensor_tensor(tmpf.t[0:m, 0:n], tmpf.t[0:m, 0:n], tmpf2.t[0:m, 0:n], ALU.add),
                 [tmpf, tmpf2], [tmpf])
            K.op("act", lambda e: e.mul(out_ap, tmpf.t[0:m, 0:n], scale), [tmpf], [out_buf])

    def layernorm_inplace(n, lnp, s, base_a, base_sh):
        for kc in range(KC):
            copy("act", hT.t[:, kc, 0:n], xs.t[:, kc, 0:n], [xs], [hT])
            K.op("dve", lambda e, kc=kc: e.tensor_tensor(merged.t[:, kc, 0:n], xs.t[:, kc, 0:n], xs.t[:, kc, 0:n], ALU.mult),
                 [xs], [merged])
        ones_sum(pst, [(hT.t[:, kc, 0:n], 128) for kc in range(KC)], n, [hT])
        K.op("act", lambda e: e.mul(mu.t[:, 0:n], pst.t[:, 0:n], 1.0 / D), [pst], [mu])
        ones_sum(pst, [(merged.t[:, kc, 0:n], 128) for kc in range(KC)], n, [merged])
        tmpf.adv()
        K.op("dve", lambda e: e.tensor_tensor(tmpf.t[:, 0:n], mu.t[:, 0:n], mu.t[:, 0:n], ALU.mult), [mu], [tmpf])
        K.op("dve", lambda e: e.scalar_tensor_tensor(rstd.t[:, 0:n], pst.t[:, 0:n], 1.0 / D, tmpf.t[:, 0:n],
                                                     ALU.mult, ALU.subtract), [pst, tmpf], [rstd])
        rsqrt_inplace(rstd, rstd.t[:, 0:n], 1.0, EPS)
        for kc in range(KC):
            tmpf.adv()
            tmpf2.adv()
            K.op("dve", lambda e, kc=kc: e.tensor_tensor(tmpf.t[:, 0:n], xs.t[:, kc, 0:n], mu.t[:, 0:n], ALU.subtract),
                 [xs, mu], [tmpf])
            K.op("dve", lambda e, kc=kc: e.tensor_tensor(tmpf2.t[:, 0:n], tmpf.t[:, 0:n], rstd.t[:, 0:n], ALU.mult),
                 [tmpf, rstd], [tmpf2])
            K.op("act", lambda e, kc=kc: e.activation(xs.t[:, kc, 0:n], tmpf2.t[:, 0:n], AF.Identity,
                                                      bias=lnp.t[:, 1, kc:kc + 1], scale=lnp.t[:, 0, kc:kc + 1]),
                 [tmpf2, lnp], [xs])
            if base_a is not None:
                K.op("dve", lambda e, kc=kc: e.tensor_scalar(hT.t[:, kc, 0:n], xs.t[:, kc, 0:n], adav(s, base_a, kc),
                                                             adav(s, base_sh, kc), ALU.mult, ALU.add), [xs, ada], [hT])

    def attention(s, n, p0, which, h, keys_buf, kT_ap_fn, kr_used, vbuf, klo, khi, q_ap, qr_ap, qbufs, scale, out_col, l):
        sq = seqs[s]
        base = (sq["past"] - sq["past"]) if which == 0 else (0)
        nk = khi - klo
        nkt = (nk + 127) // 128
        K.dma("sp", kn_v(0, nk), kT_ap_fn(klo, khi), R=[keys_buf], W=[arena])
        vsrc = vbuf.t[klo - (0 if which == 0 else sq["boff"]):khi - (0 if which == 0 else sq["boff"]), h * 128:(h + 1) * 128]
        nfull = nk // 128
        if nfull:
            vdst = A[:, OFF_V:OFF_V + nfull * 129].rearrange("p (t d) -> p t d", d=129)[:, :, 0:128]
            K.dma("sp", vdst, vsrc[0:nfull * 128, :].rearrange("(t p) d -> p t d", p=128), R=[vbuf], W=[arena])
        if nk % 128:
            kk = nk % 128
            K.dma("sp", v_v(nfull, kk, 128), vsrc[nfull * 128:nk, :], R=[vbuf], W=[arena])
        vall = A[:, OFF_V:OFF_V + nkt * 129].rearrange("p (t d) -> p t d", d=129)
        K.op("dve", lambda e: e.memset(vall[:, :, 128:129], 1.0), [], [arena])
        K.op("dve", lambda e: e.tensor_tensor(small.t[:, 2:3], qst.t[:, h, which:which + 1],
                                              kst.t[:, s, h, which:which + 1], ALU.mult), [qst, kst], [small])
        K.op("act", lambda e: e.activation(small.t[:, 2:3], small.t[:, 2:3], AF.Sqrt), [small], [small])
        if which == 0:
            K.op("dve", lambda e: e.tensor_scalar(small.t[:, 3:4], small.t[:, 2:3], -scale, None, ALU.mult), [small], [small])
        else:
            K.dma("sp", rbt.t[:], I["rbb"][l, h], W=[rbt])
            K.op("dve", lambda e: e.reduce_max(small.t[:, 4:5], rbt.t[:], AX.X), [rbt], [small])
            K.op("dve", lambda e: e.scalar_tensor_tensor(small.t[:, 3:4], small.t[:, 2:3], -scale, small.t[:, 4:5],
                                                         ALU.mult, ALU.subtract), [small], [small])
            K.op("dve", lambda e: e.tensor_tensor(small.t[:, 5:6], small.t[:, 3:4], rbt.t[:, 256:257], ALU.add),
                 [small, rbt], [small])
        nqb = (n + 127) // 128
        pot = [po[0], po[1]]
        poff = [0, 0]
        assert nqb <= 2
        live = []
        for kt in range(nkt):
            kp = klo + kt * 128
            kk = min(128, nk - kt * 128)
            iv = []
            for sub in range(2):
                ck = (kp + 64 * sub) // 64
                lo = 64 * ck - p0
                hi = n if which == 0 else 64 * (ck + 9) - p0
                iv.append((max(0, min(n, lo)), max(0, min(n, hi))))
            if kk <= 64:
                iv[1] = (0, 0)
            if iv[0][0] >= iv[0][1] and iv[1][0] >= iv[1][1]:
                continue
            live.append((kt, kp, kk, iv))
        for idx, (kt, kp, kk, iv) in enumerate(live):
            pS = pss[nxt("s")]

            def f(e, kt=kt, kk=kk, pS=pS):
                ins = e.matmul(pS.t[0:kk, 0:n], A[:, OFF_KN + kt * 128:OFF_KN + kt * 128 + kk], q_ap, start=True, stop=not kr_used)
                if kr_used:
                    ins = e.matmul(pS.t[0:kk, 0:n], A[0:64, OFF_KR + klo + kt * 128:OFF_KR + klo + kt * 128 + kk], qr_ap,
                                   start=False, stop=True)
                return ins
            K.op("pe", f, [arena] + qbufs, [pS])
            P = PT[nxt("p")]
            if which == 0:
                K.op("act", lambda e, P=P, pS=pS, kk=kk: e.activation(P.t[0:kk, 0:n], pS.t[0:kk, 0:n], AF.Exp,
                                                                      bias=small.t[0:kk, 3:4], scale=scale), [pS, small], [P])
            else:
                Dv = p0 - kp
                if Dv >= 256:
                    K.op("act", lambda e, P=P, pS=pS, kk=kk: e.activation(P.t[0:kk, 0:n], pS.t[0:kk, 0:n], AF.Exp,
                                                                          bias=small.t[0:kk, 5:6], scale=scale), [pS, small], [P])
                else:
                    di = BIAS_DS.index(Dv)
                    K.dma("sp", btile.t[:, 0:n], I["btoe"][l, h, di][:, 0:n], W=[btile])
                    tmpf.adv()
                    K.op("dve", lambda e, pS=pS, kk=kk: e.scalar_tensor_tensor(tmpf.t[0:kk, 0:n], pS.t[0:kk, 0:n], scale,
                                                                               btile.t[0:kk, 0:n], ALU.mult, ALU.add),
                         [pS, btile], [tmpf])
                    K.op("act", lambda e, P=P, kk=kk: e.activation(P.t[0:kk, 0:n], tmpf.t[0:kk, 0:n], AF.Exp,
                                                                   bias=small.t[0:kk, 3:4], scale=1.0), [tmpf, small], [P])
            for sub in range(2):
                r0, r1 = 64 * sub, min(kk, 64 * sub + 64)
                if r1 <= r0:
                    continue
                lo, hi = iv[sub]
                if lo >= hi:
                    lo, hi = 0, 0
                if lo > 0:
                    K.op("dve", lambda e, P=P, r0=r0, r1=r1, lo=lo: e.memset(P.t[r0:r1, 0:lo], 0.0), [], [P])
                if hi < n:
                    K.op("dve", lambda e, P=P, r0=r0, r1=r1, hi=hi: e.memset(P.t[r0:r1, hi:n], 0.0), [], [P])

            def g(e, P=P, kt=kt, kk=kk, idx=idx):
                ins = None
                for qb_ in range(nqb):
                    mq = min(128, n - qb_ * 128)
                    ins = e.matmul(pot[qb_].t[0:mq, poff[qb_]:poff[qb_] + 129], P.t[0:kk, qb_ * 128:qb_ * 128 + mq],
                                   v_v(kt, kk), start=(idx == 0), stop=(idx == len(live) - 1))
                return ins
            K.op("pe", g, [P, arena], [po[0], po[1]])
        for qb_ in range(nqb):
            mq = min(128, n - qb_ * 128)
            pp, off = pot[qb_], poff[qb_]
            K.op("dve", lambda e, pp=pp, off=off, mq=mq: e.reciprocal(small.t[0:mq, 8:9], pp.t[0:mq, off + 128:off + 129]),
                 [pp], [small])
            K.op("dve", lambda e, pp=pp, off=off, mq=mq: e.tensor_scalar(otok.t[0:mq, :], pp.t[0:mq, off:off + 128],
                                                                         small.t[0:mq, 8:9], None, ALU.mult), [pp, small], [otok])
            K.op("pe", lambda e, mq=mq: e.transpose(ptr.t[:, 0:mq], otok.t[0:mq, :], identb.t[0:mq, 0:mq]), [otok, identb], [ptr])
            copy("act", oabc.t[:, out_col, qb_ * 128:qb_ * 128 + mq], ptr.t[:, 0:mq], [ptr], [oabc])

    def segment(l, s, t0, n, xsrc, xsrc_buf, xdst, xdst_buf):
        sq = seqs[s]
        isP = (s == 0)
        b = s - 1
        last_layer = (l == NL - 1)
        past, bpast = sq["past"], sq["bpast"]
        p0 = past + t0
        knT, krT, vm, kbT, vbd = sq["knT"], sq["krT"], sq["vm"], sq["kbT"], sq["vb"]
        rp = I["ropeP"] if isP else I["ropeS"]
        K.dma("sp", ropeC.t[:, 0:n], rp[0][:, t0:t0 + n], W=[ropeC])
        K.dma("sp", ropeS_.t[:, 0:n], rp[1][:, t0:t0 + n], W=[ropeS_])
        K.dma("sp", xs.t[:, :, 0:n], xsrc.rearrange("(kc p) t -> p kc t", p=128)[:, :, t0:t0 + n], R=[xsrc_buf], W=[xs])
        for kc in range(KC):
            K.op("dve", lambda e, kc=kc: e.tensor_scalar(hT.t[:, kc, 0:n], xs.t[:, kc, 0:n], adav(s, 16, kc), adav(s, 0, kc),
                                                         ALU.mult, ALU.add), [xs, ada], [hT])
        hfn = lambda kc: hT.t[:, kc, 0:n]
        Win = I["w_in"]
        for j in range(4):
            pt, _, m = gemm_fm(Win, l, D, [(O_CQ + j * 128, 128)], hT, hfn, n)
            copy("act", cq.t[:, j, 0:n], pt.t[:, 0:n], [pt], [cq])
        for j in range(2):
            pt, _, m = gemm_fm(Win, l, D, [(O_CKV + j * 128, 128)], hT, hfn, n)
            copy("act", ckvr.t[:, j, 0:n], pt.t[:, 0:n], [pt], [ckvr])
        pt, pr, m = gemm_fm(Win, l, D, [(O_KR, 64)], hT, hfn, n, rot=True)
        rope_epi(pt, pr, 64, n, t0, acc.t[0:64, 0:n], acc)
        ko = (O["krT_p"][l][:, t0:t0 + n] if isP else O["krT_s"][l, b])
        K.dma("pool", ko, acc.t[0:64, 0:n], R=[acc], W=[OB["krT_p" if isP else "krT_s"]])
        tmpb.adv()
        copy("dve", tmpb.t[0:64, 0:n], acc.t[0:64, 0:n], [acc], [tmpb])
        K.dma("pool", krT.t[:, p0:p0 + n], tmpb.t[0:64, 0:n], R=[tmpb], W=[krT])
        for j in range(2):
            K.op("act", lambda e, j=j: e.activation(merged.t[:, j, 0:n], ckvr.t[:, j, 0:n], AF.Square), [ckvr], [merged])
        ones_sum(pst, [(merged.t[:, j, 0:n], 128) for j in range(2)], n, [merged])
        copy("act", rstd.t[:, 0:n], pst.t[:, 0:n], [pst], [rstd])
        rsqrt_inplace(rstd, rstd.t[:, 0:n], 1.0 / 256, EPS)
        for j in range(2):
            K.op("dve", lambda e, j=j: e.scalar_tensor_tensor(ckvr.t[:, j, 0:n], ckvr.t[:, j, 0:n], gkv.t[:, j:j + 1],
                                                              rstd.t[:, 0:n], ALU.mult, ALU.mult), [ckvr, gkv, rstd], [ckvr])
            copy("act", ckvb.t[:, j, 0:n], ckvr.t[:, j, 0:n], [ckvr], [ckvb])
        co = (O["ckvT_p"][l].rearrange("(kc p) t -> p kc t", p=128)[:, :, t0:t0 + n] if isP
              else O["ckvT_s"][l, b].rearrange("(kc p) t -> p kc t", p=128))
        K.dma("pool", co, ckvr.t[:, :, 0:n], R=[ckvr], W=[OB["ckvT_p" if isP else "ckvT_s"]])
        kv_up(l, s, p0, n, None)
        for j in range(4):
            K.op("act", lambda e, j=j: e.activation(merged.t[:, j, 0:n], cq.t[:, j, 0:n], AF.Square), [cq], [merged])
        ones_sum(pst, [(merged.t[:, j, 0:n], 128) for j in range(4)], n, [merged])
        copy("act", rstd.t[:, 0:n], pst.t[:, 0:n], [pst], [rstd])
        rsqrt_inplace(rstd, rstd.t[:, 0:n], 1.0 / 512, EPS)
        for j in range(4):
            K.op("dve", lambda e, j=j: e.scalar_tensor_tensor(cqn.t[:, j, 0:n], cq.t[:, j, 0:n], gq.t[:, j:j + 1],
                                                              rstd.t[:, 0:n], ALU.mult, ALU.mult), [cq, gq, rstd], [cqn])
        cfn = lambda kc: cqn.t[:, kc, 0:n]
        K.op("dve", lambda e: e.memset(qst.t[:], 0.0), [], [qst])
        for h in range(8):
            pt, _, m = gemm_fm(I["w_uq"], l, 512, [(h * 192, 128)], cqn, cfn, n)
            tmpb.adv()
            copy("act", tmpb.t[:, 0:n], pt.t[:, 0:n], [pt], [tmpb])
            K.dma("pool", qscr.t[1, h, :, 0:n], tmpb.t[:, 0:n], R=[tmpb], W=[qscr])
            pt2, pr2, m = gemm_fm(I["w_uq"], l, 512, [(h * 192 + 128, 64)], cqn, cfn, n, rot=True)
            rope_epi(pt2, pr2, 64, n, t0, merged.t[0:64, 0, 0:n], merged)
            K.dma("pool", qscr.t[2, h, 0:64, 0:n], merged.t[0:64, 0, 0:n], R=[merged], W=[qscr])
            K.op("act", lambda e: e.activation(merged.t[:, 1, 0:n], tmpb.t[:, 0:n], AF.Square), [tmpb], [merged])
            K.op("act", lambda e: e.activation(merged.t[0:64, 2, 0:n], merged.t[0:64, 0, 0:n], AF.Square), [merged], [merged])
            ones_sum(pst, [(merged.t[:, 1, 0:n], 128), (merged.t[0:64, 2, 0:n], 64)], n, [merged])
            K.op("dve", lambda e, h=h: e.reduce_max(qst.t[:, h, 0:1], pst.t[:, 0:n], AX.X), [pst], [qst])
        for h in range(8):
            pt, _, m = gemm_fm(Win, l, D, [(O_QB + h * 128, 128)], hT, hfn, n)
            tmpb.adv()
            copy("act", tmpb.t[:, 0:n], pt.t[:, 0:n], [pt], [tmpb])
            K.dma("pool", qscr.t[0, h, :, 0:n], tmpb.t[:, 0:n], R=[tmpb], W=[qscr])
            K.op("act", lambda e: e.activation(merged.t[:, 1, 0:n], tmpb.t[:, 0:n], AF.Square), [tmpb], [merged])
            ones_sum(pst, [(merged.t[:, 1, 0:n], 128)], n, [merged])
            K.op("dve", lambda e, h=h: e.reduce_max(qst.t[:, h, 1:2], pst.t[:, 0:n], AX.X), [pst], [qst])
            pt, _, m = gemm_fm(Win, l, D, [(O_KB + h * 128, 128)], hT, hfn, n)
            tmpb.adv()
            copy("act", tmpb.t[:, 0:n], pt.t[:, 0:n], [pt], [tmpb])
            K.dma("pool", kbT.t[h, :, bpast + t0:bpast + t0 + n], tmpb.t[:, 0:n], R=[tmpb], W=[kbT])
            kstat(s, h, 1, [(tmpb.t[:, 0:n], 128)], n, [tmpb])
            if isP:
                if t0 + n > sq["T"] - KEEP:
                    tmpf.adv()
                    copy("act", tmpf.t[:, 0:n], pt.t[:, 0:n], [pt], [tmpf])
                    o0 = t0 - (sq["T"] - KEEP)
                    K.dma("pool", O["bkT_p"][l, h][:, o0:o0 + n], tmpf.t[:, 0:n], R=[tmpf], W=[OB["bkT_p"]])
            else:
                tmpf.adv()
                copy("act", tmpf.t[:, 0:n], pt.t[:, 0:n], [pt], [tmpf])
                K.dma("pool", O["bkT_s"][l, b, h], tmpf.t[:, 0:n], R=[tmpf], W=[OB["bkT_s"]])
        for half in range(2):
            def epi(pt, tb, nt, c, half=half):
                tmpb.adv()
                copy("act", tmpb.t[0:nt, 0:c], pt.t[0:nt, 0:c], [pt], [tmpb])
                r0 = bpast + t0 + tb * 128
                K.dma("pool", vbd.t[r0:r0 + nt, half * 512:half * 512 + 512], tmpb.t[0:nt, 0:c], R=[tmpb], W=[vbd])
                if isP:
                    if t0 + n > sq["T"] - KEEP:
                        tmpf.adv()
                        copy("act", tmpf.t[0:nt, 0:c], pt.t[0:nt, 0:c], [pt], [tmpf])
                        o0 = t0 - (sq["T"] - KEEP) + tb * 128
                        K.dma("pool", O["bv_p"][l][o0:o0 + nt, half * 512:half * 512 + 512], tmpf.t[0:nt, 0:c],
                              R=[tmpf], W=[OB["bv_p"]])
                else:
                    tmpf.adv()
                    copy("act", tmpf.t[0:nt, 0:c], pt.t[0:nt, 0:c], [pt], [tmpf])
                    K.dma("pool", O["bv_s"][l, b][tb * 128:tb * 128 + nt, half * 512:half * 512 + 512], tmpf.t[0:nt, 0:c],
                          R=[tmpf], W=[OB["bv_s"]])
            gemm_tm(Win, l, D, [(O_VB + half * 512, 512)], hT, lambda kc, c0, nt: hT.t[:, kc, c0:c0 + nt], n, epi)
        for h in range(8):
            pt, pr, m = gemm_fm(Win, l, D, [(O_QR + h * 64, 64)], hT, hfn, n, rot=True)
            rope_epi(pt, pr, 64, n, t0, qret.t[:, h, 0:n], qret, scale=64 ** -0.5)
            pt, pr, m = gemm_fm(Win, l, D, [(O_KRR + h * 64, 64)], hT, hfn, n, rot=True)
            rope_epi(pt, pr, 64, n, t0, kret.t[:, h, 0:n], kret)
        for half in range(2):
            def epi(pt, tb, nt, c, half=half):
                copy("act", vr.t[0:nt, tb, half * 512:half * 512 + 512], pt.t[0:nt, 0:c], [pt], [vr])
            gemm_tm(Win, l, D, [(O_VR + half * 512, 512)], hT, lambda kc, c0, nt: hT.t[:, kc, c0:c0 + nt], n, epi)
        for j in range(8):
            pt, _, m = gemm_fm(Win, l, D, [(O_GR + j * 128, 128)], hT, hfn, n)
            K.op("act", lambda e, pt=pt, j=j: e.activation(sgr.t[:, j, 0:n], pt.t[:, 0:n], AF.Silu), [pt], [sgr])
        for j in range(48):
            pt, _, m = gemm_fm(Win, l, D, [(O_GA + j * 128, 128)], hT, hfn, n)
            tmpb.adv()
            K.op("act", lambda e, pt=pt: e.activation(tmpb.t[:, 0:n], pt.t[:, 0:n], AF.Sigmoid), [pt], [tmpb])
            K.dma("pool", gates.t[j, :, 0:n], tmpb.t[:, 0:n], R=[tmpb], W=[gates])
        khi = p0 + n
        K.dma("sp", kr_v(0, khi), krT.t[:, 0:khi], R=[krT], W=[arena])
        for h in range(8):
            K.dma("sp", merged.t[:, 4, 0:n], qscr.t[1, h, :, 0:n], R=[qscr], W=[merged])
            K.dma("sp", merged.t[0:64, 5, 0:n], qscr.t[2, h, 0:64, 0:n], R=[qscr], W=[merged])
            attention(s, n, p0, 0, h, knT, lambda a, b_, h=h: knT.t[h, :, a:b_], True, vm, 0, khi,
                      merged.t[:, 4, 0:n], merged.t[0:64, 5, 0:n], [merged], SC_MLA, h, l)
        sq["boff"] = past - bpast
        blo = max(sq["boff"], p0 - 512)
        for h in range(8):
            K.dma("sp", merged.t[:, 4, 0:n], qscr.t[0, h, :, 0:n], R=[qscr], W=[merged])
            attention(s, n, p0, 1, h, kbT, lambda a, b_, h=h: kbT.t[h, :, a - sq["boff"]:b_ - sq["boff"]], False, vbd,
                      blo, khi, merged.t[:, 4, 0:n], None, [merged], SC_B, 8 + h, l)
        Lc = 128 if n >= 128 else n
        dki = 0 if Lc == 128 else 1
        for c in range(n // Lc):
            c0 = c * Lc
            for h in range(8):
                gam = 1.0 - 2.0 ** (-5 - h)
                gL = float(gam ** Lc)
                K.op("pe", lambda e, h=h, c0=c0: e.matmul(pss[0].t[0:Lc, 0:Lc], kret.t[:, h, c0:c0 + Lc], qret.t[:, h, c0:c0 + Lc],
                                                          start=True, stop=True), [kret, qret], [pss[0]])
                K.op("dve", lambda e, h=h: e.tensor_tensor(atm.t[0:Lc, 0:Lc], pss[0].t[0:Lc, 0:Lc], dm.t[0:Lc, h, 0:Lc], ALU.mult),
                     [pss[0], dm], [atm])
                K.op("dve", lambda e, h=h, c0=c0: e.tensor_tensor(qd.t[:, 0:Lc], qret.t[:, h, c0:c0 + Lc], dq.t[:, h, 0:Lc], ALU.mult),
                     [qret, dq], [qd])
                copy("act", Sbf.t[:, h, :], Sst.t[:, h, :], [Sst], [Sbf])

                def fo(e, h=h, c=c):
                    e.matmul(pss[1].t[0:Lc, 0:128], atm.t[0:Lc, 0:Lc], vr.t[0:Lc, c, h * 128:(h + 1) * 128], start=True, stop=False)
                    return e.matmul(pss[1].t[0:Lc, 0:128], qd.t[:, 0:Lc], Sbf.t[:, h, :], start=False, stop=True)
                K.op("pe", fo, [atm, vr, qd, Sbf], [pss[1]])
                K.op("pe", lambda e, h=h, c0=c0: e.transpose(ptr.t[0:Lc, 0:64], kret.t[:, h, c0:c0 + Lc], identb.t[0:64, 0:64]),
                     [kret, identb], [ptr])
                K.op("dve", lambda e, h=h: e.tensor_scalar(kdec.t[0:Lc, :], ptr.t[0:Lc, 0:64], dk.t[0:Lc, dki, h:h + 1], None, ALU.mult),
                     [ptr, dk], [kdec])
                K.op("pe", lambda e, h=h, c=c: e.matmul(pst.t[0:64, 0:128], kdec.t[0:Lc, :], vr.t[0:Lc, c, h * 128:(h + 1) * 128],
                                                        start=True, stop=True), [kdec, vr], [pst])
                K.op("dve", lambda e, h=h: e.scalar_tensor_tensor(Sst.t[:, h, :], Sst.t[:, h, :], gL, pst.t[0:64, 0:128],
                                                                  ALU.mult, ALU.add), [Sst, pst], [Sst])
                tmpf.adv()
                tmpf2.adv()
                K.op("act", lambda e: e.activation(tmpf.t[0:Lc, 0:128], pss[1].t[0:Lc, 0:128], AF.Identity,
                                                   accum_out=small.t[0:Lc, 10:11]), [pss[1]], [tmpf, small])
                K.op("act", lambda e: e.activation(tmpf2.t[0:Lc, 0:128], pss[1].t[0:Lc, 0:128], AF.Square,
                                                   accum_out=small.t[0:Lc, 11:12]), [pss[1]], [tmpf2, small])
                K.op("dve", lambda e: e.tensor_scalar(small.t[0:Lc, 10:12], small.t[0:Lc, 10:12], 1.0 / 128, None, ALU.mult),
                     [small], [small])
                K.op("dve", lambda e: e.tensor_tensor(small.t[0:Lc, 12:13], small.t[0:Lc, 10:11], small.t[0:Lc, 10:11], ALU.mult),
                     [small], [small])
                K.op("dve", lambda e: e.tensor_tensor(small.t[0:Lc, 13:14], small.t[0:Lc, 11:12], small.t[0:Lc, 12:13], ALU.subtract),
                     [small], [small])
                rsqrt_inplace(small, small.t[0:Lc, 13:14], 1.0, EPS)
                K.op("dve", lambda e: e.tensor_scalar(onb.t[0:Lc, :], tmpf.t[0:Lc, 0:128], small.t[0:Lc, 10:11], small.t[0:Lc, 13:14],
                                                      ALU.subtract, ALU.mult), [tmpf, small], [onb])
                K.op("pe", lambda e: e.transpose(ptr.t[:, 0:Lc], onb.t[0:Lc, :], identb.t[0:Lc, 0:Lc]), [onb, identb], [ptr])
                K.op("dve", lambda e, h=h, c0=c0: e.scalar_tensor_tensor(oabc.t[:, 16 + h, c0:c0 + Lc], ptr.t[:, 0:Lc], grn.t[:, h:h + 1],
                                                                         sgr.t[:, h, c0:c0 + Lc], ALU.mult, ALU.mult),
                     [ptr, grn, sgr], [oabc])
        for j in range(16):
            for bi, Wn in enumerate(("w_pa", "w_pb", "w_pc")):
                pt, _, m = gemm_fm(I[Wn], l, 1024, [(j * 128, 128)], oabc, lambda kc, bi=bi: oabc.t[:, bi * 8 + kc, 0:n], n)
                tmpb.adv()
                K.dma("sp", tmpb.t[:, 0:n], gates.t[bi * 16 + j, :, 0:n], R=[gates], W=[tmpb])
                if bi == 0:
                    K.op("dve", lambda e, pt=pt: e.tensor_tensor(acc.t[:, 0:n], pt.t[:, 0:n], tmpb.t[:, 0:n], ALU.mult),
                         [pt, tmpb], [acc])
                else:
                    tmpf.adv()
                    K.op("dve", lambda e, pt=pt: e.tensor_tensor(tmpf.t[:, 0:n], pt.t[:, 0:n], tmpb.t[:, 0:n], ALU.mult),
                         [pt, tmpb], [tmpf])
                    K.op("dve", lambda e: e.tensor_tensor(acc.t[:, 0:n], acc.t[:, 0:n], tmpf.t[:, 0:n], ALU.add),
                         [acc, tmpf], [acc])
            copy("act", merged.t[:, j, 0:n], acc.t[:, 0:n], [acc], [merged])
        for j in range(16):
            pt, _, m = gemm_fm(I["w_o"], l, D, [(j * 128, 128)], merged, lambda kc: merged.t[:, kc, 0:n], n)
            tmpf.adv()
            K.op("dve", lambda e, pt=pt, j=j: e.tensor_scalar(tmpf.t[:, 0:n], pt.t[:, 0:n], adav(s, 32, j), None, ALU.mult),
                 [pt, ada], [tmpf])
            K.op("dve", lambda e, j=j: e.scalar_tensor_tensor(xs.t[:, j, 0:n], xs.t[:, j, 0:n], ALPHA, tmpf.t[:, 0:n],
                                                              ALU.mult, ALU.add), [xs, tmpf], [xs])
        layernorm_inplace(n, ln1, s, 64, 48)
        for jb in range(FC):
            pa, _, m = gemm_fm(I["w_fa"], l, D, [(jb * 128, 128)], hT, hfn, n)
            ah.adv()
            tmpf.adv()
            tmpf2.adv()
            copy("dve", ah.t[:, 0:2], HS.t[:, jb, :], [HS], [ah])
            copy("act", ah.t[:, 2:2 + n], pa.t[:, 0:n], [pa, ah], [ah])
            copy("dve", HS.t[:, jb, :], ah.t[:, n:n + 2], [ah], [HS])
            K.op("dve", lambda e, jb=jb: e.tensor_scalar(tmpf.t[:, 0:n], ah.t[:, 0:n], cw.t[:, jb, 0:1], cw.t[:, jb, 3:4],
                                                         ALU.mult, ALU.add), [ah, cw], [tmpf])
            K.op("dve", lambda e, jb=jb: e.scalar_tensor_tensor(tmpf.t[:, 0:n], ah.t[:, 1:1 + n], cw.t[:, jb, 1:2], tmpf.t[:, 0:n],
                                                                ALU.mult, ALU.add), [ah, cw, tmpf], [tmpf])
            K.op("dve", lambda e, jb=jb: e.scalar_tensor_tensor(tmpf.t[:, 0:n], ah.t[:, 2:2 + n], cw.t[:, jb, 2:3], tmpf.t[:, 0:n],
                                                                ALU.mult, ALU.add), [ah, cw, tmpf], [tmpf])
            K.op("act", lambda e: e.activation(tmpf2.t[:, 0:n], tmpf.t[:, 0:n], AF.Gelu), [tmpf], [tmpf2])
            pb, _, m = gemm_fm(I["w_fb"], l, D, [(jb * 128, 128)], hT, hfn, n)
            K.op("dve", lambda e, pb=pb, jb=jb: e.tensor_tensor(u_v(jb, n), tmpf2.t[:, 0:n], pb.t[:, 0:n], ALU.mult),
                 [tmpf2, pb], [arena])
        for j in range(16):
            pt, _, m = gemm_fm(I["w_fd"], l, DFF, [(j * 128, 128)], arena, lambda kc: u_v(kc, n), n)
            tmpf.adv()
            K.op("dve", lambda e, pt=pt, j=j: e.tensor_scalar(tmpf.t[:, 0:n], pt.t[:, 0:n], adav(s, 80, j), None, ALU.mult),
                 [pt, ada], [tmpf])
            K.op("dve", lambda e, j=j: e.scalar_tensor_tensor(xs.t[:, j, 0:n], xs.t[:, j, 0:n], ALPHA, tmpf.t[:, 0:n],
                                                              ALU.mult, ALU.add), [xs, tmpf], [xs])
        layernorm_inplace(n, ln2, s, None, None)
        K.dma("pool", xdst.rearrange("(kc p) t -> p kc t", p=128)[:, :, t0:t0 + n], xs.t[:, :, 0:n], R=[xs], W=[xdst_buf])

    for l in range(NL):
        layer(l)
    for q in ("sp", "pool", "pe", "act"):
        for sl in K.slots[q]:
            if sl.val:
                K.E["pool"].h.wait_ge(sl.sem, sl.val)
    for en in ("pe", "act", "dve"):
        e = K.E[en]
        if e.cnt:
            K.E["pool"].h.wait_ge(e.sem, e.cnt)
    return nc, es


def _consts(TP, PAST):
    half = 32
    inv = (10000.0 ** (-np.arange(half, dtype=np.float32) / half)).astype(np.float32)

    def tab(pos):
        ang = pos.astype(np.float32)[:, None] * inv[None, :]
        c, s_ = np.cos(ang).astype(np.float32), np.sin(ang).astype(np.float32)
        return np.stack([np.concatenate([c, c], 1).T, np.concatenate([s_, s_], 1).T]).astype(np.float32)
    ropeP = np.ascontiguousarray(tab(np.arange(TP)))
    ropeS = np.ascontiguousarray(tab(PAST + np.arange(16)))
    lg = np.log1p(-np.exp2(-5.0 - np.arange(8, dtype=np.float32))).astype(np.float32)
    idx = np.arange(128, dtype=np.float32)
    diff = idx[None, :] - idx[:, None]
    dm = np.where(diff[None] >= 0, np.exp(np.maximum(diff, 0.0)[None] * lg[:, None, None]), 0.0).astype(np.float32)
    dq = np.exp((idx[None, :] + 1.0) * lg[:, None]).astype(np.float32)
    dq = np.ascontiguousarray(np.broadcast_to(dq[None], (64, 8, 128))).astype(np.float32)
    dk = np.stack([np.exp((127.0 - idx)[:, None] * lg[None, :]),
                   np.exp((15.0 - idx)[:, None] * lg[None, :])]).astype(np.float32)
    dk[1, 16:] = 0.0
    return dict(ropeP=ropeP, ropeS=ropeS, dm=dm, dq=dq, dk=np.ascontiguousarray(dk),
                ident=np.eye(128, dtype=np.float32))


def _vec128(v):
    L_ = v.shape[0]
    return np.ascontiguousarray(v.reshape(L_, -1, 128).transpose(0, 2, 1))


def host_inputs(inp, core, TP, NSB, NL, PAST=1024, BPAST=512):
    f = np.float32
    pb = core % inp["x_prompt"].shape[0]
    sb = [(core * NSB + i) % inp["x_sample"].shape[0] for i in range(NSB)]
    m = {}
    m["xT_p"] = np.ascontiguousarray(inp["x_prompt"][pb, :TP].T)
    m["xT_s"] = np.ascontiguousarray(inp["x_sample"][sb].transpose(0, 2, 1))
    c = np.concatenate([inp["c_prompt"][pb:pb + 1], inp["c_sample"][sb]], 0)
    m["cT"] = np.ascontiguousarray(c.reshape(-1, KC, 128).transpose(2, 1, 0))
    m["ckvT_c"] = np.ascontiguousarray(inp["cache_mla_ckv"][:NL][:, sb].transpose(0, 1, 3, 2))
    m["krT_c"] = np.ascontiguousarray(inp["cache_mla_krope"][:NL][:, sb].transpose(0, 1, 3, 2))
    m["bkT_c"] = np.ascontiguousarray(inp["cache_band_k"][:NL][:, sb].transpose(0, 1, 3, 4, 2))
    m["bv_c"] = np.ascontiguousarray(inp["cache_band_v"][:NL][:, sb].reshape(NL, NSB, BPAST, 1024))
    m["sret"] = np.ascontiguousarray(inp["state_ret"][:NL][:, sb].transpose(0, 1, 3, 2, 4))
    sc = inp["state_conv"][:NL][:, sb]
    m["sconvT"] = np.ascontiguousarray(sc.reshape(NL, NSB, 2, FC, 128).transpose(0, 1, 4, 3, 2))
    return m


def shared_inputs(inp, TP, NL, PAST=1024):
    m = {}
    for k in ("w_ada", "w_in", "w_uq", "w_ukv", "w_o"):
        m[k] = np.ascontiguousarray(inp[k][:NL])
    m["w_pa"] = np.ascontiguousarray(inp["w_branch_a"][:NL])
    m["w_pb"] = np.ascontiguousarray(inp["w_branch_b"][:NL])
    m["w_pc"] = np.ascontiguousarray(inp["w_branch_c"][:NL])
    m["w_fa"] = np.ascontiguousarray(inp["w_ff_a"][:NL])
    m["w_fb"] = np.ascontiguousarray(inp["w_ff_b"][:NL])
    m["w_fd"] = np.ascontiguousarray(inp["w_ff_down"][:NL])
    m["b_adaT"] = _vec128(inp["b_ada"][:NL])
    m["g_q"] = _vec128(inp["g_q_lora"][:NL])
    m["g_kv"] = _vec128(inp["g_kv_lora"][:NL])
    m["g_rn"] = _vec128(inp["g_ret_norm"][:NL])
    m["ln1"] = np.ascontiguousarray(np.stack([_vec128(inp["ln1_g"][:NL]), _vec128(inp["ln1_b"][:NL])], 2))
    m["ln2"] = np.ascontiguousarray(np.stack([_vec128(inp["ln2_g"][:NL]), _vec128(inp["ln2_b"][:NL])], 2))
    cwf = np.concatenate([inp["conv_w"][:NL], inp["conv_b"][:NL][:, None, :]], 1)
    m["cwT"] = np.ascontiguousarray(cwf.reshape(NL, 4, FC, 128).transpose(0, 3, 2, 1))
    rb = inp["rel_bias"][:NL]
    m["rbb"] = np.ascontiguousarray(np.broadcast_to(rb[:, :, None, :], (NL, 8, 128, 257)))
    r = np.arange(128)[:, None]
    i = np.arange(512)[None, :]
    idxs = np.stack([np.clip(Dv + i - r, -128, 128) + 128 for Dv in BIAS_DS])
    m["btoe"] = np.ascontiguousarray(rb[:, :, idxs])
    m.update(_consts(TP, PAST))
    return m


def assemble(res, TP, NSB, NL, nP, nS):
    KEEP = min(512, TP)
    P = res[:nP]
    ncs = nS // NSB

    def cat_s(f_):
        return np.concatenate([f_(res[c]) for c in range(ncs)], 1)
    y_p = np.stack([r["yT_p"].T for r in P])
    y_s = np.concatenate([r["yT_s"].transpose(0, 2, 1) for r in res[:ncs]], 0)
    ckv_p = np.stack([r["ckvT_p"].transpose(0, 2, 1) for r in P], 1)
    kr_p = np.stack([r["krT_p"].transpose(0, 2, 1) for r in P], 1)
    bk_p = np.stack([r["bkT_p"].transpose(0, 3, 1, 2) for r in P], 1)
    bv_p = np.stack([r["bv_p"].reshape(NL, KEEP, 8, 128) for r in P], 1)
    ret_p = np.stack([r["ret_p"].transpose(0, 2, 1, 3) for r in P], 1)
    conv_p = np.stack([r["convT_p"].transpose(0, 3, 2, 1).reshape(NL, 2, DFF) for r in P], 1)
    ckv_s = cat_s(lambda r: r["ckvT_s"].transpose(0, 1, 3, 2))
    kr_s = cat_s(lambda r: r["krT_s"].transpose(0, 1, 3, 2))
    bk_s = cat_s(lambda r: r["bkT_s"].transpose(0, 1, 4, 2, 3))
    bv_s = cat_s(lambda r: r["bv_s"].reshape(NL, NSB, 16, 8, 128))
    ret_s = cat_s(lambda r: r["ret_s"].transpose(0, 1, 3, 2, 4))
    conv_s = cat_s(lambda r: r["convT_s"].transpose(0, 1, 4, 3, 2).reshape(NL, NSB, 2, DFF))
    outs = (y_p, y_s, ckv_p, kr_p, bk_p, bv_p, ret_p, conv_p, ckv_s, kr_s, bk_s, bv_s, ret_s, conv_s)
    return tuple(np.ascontiguousarray(o, dtype=np.float32) for o in outs)


def kernel(**inputs):
    inp = {k: np.asarray(v) for k, v in inputs.items()}
    NL = inp["w_in"].shape[0]
    TP = inp["x_prompt"].shape[1]
    nP = inp["x_prompt"].shape[0]
    nS = inp["x_sample"].shape[0]
    NSB = nS // 8
    PAST = inp["cache_mla_ckv"].shape[2]
    nc, es = build(TP, NSB, NL, PAST=PAST, BPAST=inp["cache_band_k"].shape[2])
    shared = shared_inputs(inp, TP, NL, PAST)
    in_maps = []
    for c in range(8):
        m = dict(shared)
        m.update(host_inputs(inp, c, TP, NSB, NL, PAST, inp["cache_band_k"].shape[2]))
        in_maps.append(m)
    res = run_bass_kernel_spmd(nc, in_maps, core_ids=list(range(8)))
    es.close()
    return assemble(res.results, TP, NSB, NL, nP, nS)
```

```python
from contextlib import ExitStack
import numpy as np
import concourse.bass as bass
import concourse.mybir as mybir
from concourse.bass_utils import run_bass_kernel_spmd

F32 = mybir.dt.float32
BF16 = mybir.dt.bfloat16
AF = mybir.ActivationFunctionType
ALU = mybir.AluOpType
AX = mybir.AxisListType

D = 2048
KC = 16
NIN = 13120
DFF = 5632
FC = 44
ALPHA = 4 ** 0.25
EPS = 1e-5
O_CQ, O_CKV, O_KR, O_QB, O_KB, O_VB, O_QR, O_KRR, O_VR, O_GR, O_GA, O_GB, O_GC = (
    0, 512, 768, 832, 1856, 2880, 3904, 4416, 4928, 5952, 6976, 9024, 11072)
SC_MLA = 192 ** -0.5
SC_B = 128 ** -0.5
BIAS_DS = (128, 0, -128, -256, -384)


class Buf:
    def __init__(self, t=None):
        self.t = t
        self.w = {}
        self.r = {}


class Ring:
    def __init__(self, bufs):
        self.bufs, self.i = bufs, 0

    @property
    def t(self):
        return self.bufs[self.i].t

    @property
    def cur(self):
        return self.bufs[self.i]

    def adv(self):
        self.i = (self.i + 1) % len(self.bufs)


def _res(bs):
    return [b.cur if isinstance(b, Ring) else b for b in bs]


class Eng:
    def __init__(self, name, h, sem):
        self.name, self.h, self.sem, self.cnt, self.seen = name, h, sem, 0, {}


class Slot:
    def __init__(self, sem, key):
        self.sem, self.val, self.key = sem, 0, key


class KB:
    def __init__(self, nc, es):
        self.nc, self.es = nc, es
        self.E = {}
        for name, h in (("pe", nc.tensor), ("act", nc.scalar), ("dve", nc.vector),
                        ("pool", nc.gpsimd), ("sp", nc.sync)):
            self.E[name] = Eng(name, h, es.enter_context(nc.semaphore("s_" + name)))
        self.slots = {q: [Slot(es.enter_context(nc.semaphore(f"d_{q}{i}")), f"d_{q}{i}") for i in range(10)]
                      for q in ("sp", "pool", "pe", "act")}
        self.slot_i = {"sp": 0, "pool": 0, "pe": 0, "act": 0}
        self.nt = 0

    def sb(self, shape, dt, name=None):
        self.nt += 1
        return Buf(self.es.enter_context(self.nc.sbuf_tensor("sb_" + (name or f"t{self.nt}"), list(shape), dt)))

    def ps(self, shape, dt, name=None):
        self.nt += 1
        return Buf(self.es.enter_context(self.nc.psum_tensor("ps_" + (name or f"p{self.nt}"), list(shape), dt)))

    def _wait(self, e, R, W, is_dma=False):
        R, W = _res(R), _res(W)
        deps = {}
        for b in R:
            for k, v in b.w.items():
                if deps.get(k, (None, 0))[1] < v[1]:
                    deps[k] = v
        for b in W:
            for dd in (b.w, b.r):
                for k, v in dd.items():
                    if deps.get(k, (None, 0))[1] < v[1]:
                        deps[k] = v
        for k, (sem, val) in deps.items():
            if e.name == "pe" and k == "s_pe" and not is_dma:
                continue
            if e.seen.get(k, 0) < val:
                e.h.wait_ge(sem, val)
                e.seen[k] = val

    def _mark(self, key, tok, R, W):
        R, W = _res(R), _res(W)
        for b in W:
            b.w[key] = tok
            b.r = {}
        for b in R:
            if b not in W:
                b.r[key] = tok

    def op(self, en, fn, R=(), W=()):
        e = self.E[en]
        self._wait(e, R, W)
        ins = fn(e.h)
        e.cnt += 1
        ins.then_inc(e.sem, 1)
        self._mark("s_" + en, (e.sem, e.cnt), R, W)

    def dma(self, q, out, in_, R=(), W=()):
        e = self.E[q]
        sl = self.slots[q][self.slot_i[q]]
        self.slot_i[q] = (self.slot_i[q] + 1) % len(self.slots[q])
        key = sl.key
        if sl.val and e.seen.get(key, 0) < sl.val:
            e.h.wait_ge(sl.sem, sl.val)
            e.seen[key] = sl.val
        self._wait(e, R, W, is_dma=True)
        ins = e.h.dma_start(out=out, in_=in_)
        sl.val += 16
        ins.then_inc(sl.sem, 16)
        self._mark(key, (sl.sem, sl.val), R, W)


def build(TP, NSB, NL, PAST=1024, BPAST=512, SEG=256):
    nc = bass.Bass("TRN2", target_bir_lowering=False)
    es = ExitStack()
    K = KB(nc, es)
    NSEQ = 1 + NSB
    KEEP = min(512, TP)
    TS = 16

    def din(name, shape):
        return nc.dram_tensor(name, list(shape), F32, kind="ExternalInput")

    def dout(name, shape):
        return nc.dram_tensor(name, list(shape), F32, kind="ExternalOutput")

    def dscr(name, shape, dt=BF16):
        return Buf(nc.dram_tensor(name, list(shape), dt))

    I = {}
    I["xT_p"] = din("xT_p", [D, TP])
    I["xT_s"] = din("xT_s", [NSB, D, TS])
    I["cT"] = din("cT", [128, KC, NSEQ])
    I["ckvT_c"] = din("ckvT_c", [NL, NSB, 256, PAST])
    I["krT_c"] = din("krT_c", [NL, NSB, 64, PAST])
    I["bkT_c"] = din("bkT_c", [NL, NSB, 8, 128, BPAST])
    I["bv_c"] = din("bv_c", [NL, NSB, BPAST, 1024])
    I["sret"] = din("sret", [NL, NSB, 64, 8, 128])
    I["sconvT"] = din("sconvT", [NL, NSB, 128, FC, 2])
    I["w_ada"] = din("w_ada", [NL, D, 6 * D])
    I["b_adaT"] = din("b_adaT", [NL, 128, 96])
    I["w_in"] = din("w_in", [NL, D, NIN])
    I["g_q"] = din("g_q", [NL, 128, 4])
    I["g_kv"] = din("g_kv", [NL, 128, 2])
    I["w_uq"] = din("w_uq", [NL, 512, 1536])
    I["w_ukv"] = din("w_ukv", [NL, 256, 2048])
    I["btoe"] = din("btoe", [NL, 8, len(BIAS_DS), 128, 512])
    I["rbb"] = din("rbb", [NL, 8, 128, 257])
    I["g_rn"] = din("g_rn", [NL, 128, 8])
    I["w_pa"] = din("w_pa", [NL, 1024, D])
    I["w_pb"] = din("w_pb", [NL, 1024, D])
    I["w_pc"] = din("w_pc", [NL, 1024, D])
    I["w_o"] = din("w_o", [NL, D, D])
    I["ln1"] = din("ln1", [NL, 128, 2, KC])
    I["w_fa"] = din("w_fa", [NL, D, DFF])
    I["w_fb"] = din("w_fb", [NL, D, DFF])
    I["cwT"] = din("cwT", [NL, 128, FC, 4])
    I["w_fd"] = din("w_fd", [NL, DFF, D])
    I["ln2"] = din("ln2", [NL, 128, 2, KC])
    I["ropeP"] = din("ropeP", [2, 64, TP])
    I["ropeS"] = din("ropeS", [2, 64, TS])
    I["ident"] = din("ident", [128, 128])
    I["dm"] = din("dm", [8, 128, 128])
    I["dq"] = din("dq", [64, 8, 128])
    I["dk"] = din("dk", [2, 128, 8])

    O = {}
    O["yT_p"] = dout("yT_p", [D, TP])
    O["yT_s"] = dout("yT_s", [NSB, D, TS])
    O["ckvT_p"] = dout("ckvT_p", [NL, 256, TP])
    O["krT_p"] = dout("krT_p", [NL, 64, TP])
    O["bkT_p"] = dout("bkT_p", [NL, 8, 128, KEEP])
    O["bv_p"] = dout("bv_p", [NL, KEEP, 1024])
    O["ret_p"] = dout("ret_p", [NL, 64, 8, 128])
    O["convT_p"] = dout("convT_p", [NL, 128, FC, 2])
    O["ckvT_s"] = dout("ckvT_s", [NL, NSB, 256, TS])
    O["krT_s"] = dout("krT_s", [NL, NSB, 64, TS])
    O["bkT_s"] = dout("bkT_s", [NL, NSB, 8, 128, TS])
    O["bv_s"] = dout("bv_s", [NL, NSB, TS, 1024])
    O["ret_s"] = dout("ret_s", [NL, NSB, 64, 8, 128])
    O["convT_s"] = dout("convT_s", [NL, NSB, 128, FC, 2])
    OB = {k: Buf(v) for k, v in O.items()}

    seqs = []
    for s in range(NSEQ):
        if s == 0:
            sq = dict(T=TP, past=0, bpast=0, seg=min(SEG, TP))
        else:
            sq = dict(T=TS, past=PAST, bpast=BPAST, seg=TS)
        sq["TK"] = sq["past"] + sq["T"]
        sq["TB"] = sq["bpast"] + sq["T"]
        sq["knT"] = dscr(f"knT{s}", [8, 128, sq["TK"]])
        sq["krT"] = dscr(f"krT{s}", [64, sq["TK"]])
        sq["vm"] = dscr(f"vm{s}", [sq["TK"], 1024])
        sq["kbT"] = dscr(f"kbT{s}", [8, 128, sq["TB"]])
        sq["vb"] = dscr(f"vb{s}", [sq["TB"], 1024])
        sq["x1"] = dscr(f"x1_{s}", [D, sq["T"]], F32)
        seqs.append(sq)
    gates = dscr("gates", [48, 128, 512])
    qscr = dscr("qscr", [3, 8, 128, 512])

    sb, ps = K.sb, K.ps
    xs = sb([128, KC, SEG], F32, "xs")
    hT = sb([128, KC, SEG], BF16, "hT")
    wst = [sb([128, KC, 128], F32, f"wst{i}") for i in range(1)]
    wbf = [sb([128, KC, 128], BF16, f"wbf{i}") for i in range(4)]
    wrotc = [sb([128, KC, 128], BF16, f"wrotc{i}") for i in range(1)]
    wst4 = sb([128, 2, 512], F32, "wst4")
    wbf4 = [sb([128, KC, 512], BF16, f"wbf4_{i}") for i in range(1)]
    cq = sb([128, 4, SEG], F32, "cq")
    ckvr = sb([128, 2, SEG], F32, "ckvr")
    cqn = sb([128, 4, SEG], BF16, "cqn")
    ckvb = sb([128, 2, SEG], BF16, "ckvb")
    qret = sb([64, 8, SEG], BF16, "qret")
    kret = sb([64, 8, SEG], BF16, "kret")
    vr = sb([128, SEG // 128, 1024], BF16, "vr")
    sgr = sb([128, 8, SEG], BF16, "sgr")
    arena = Buf(es.enter_context(nc.sbuf_tensor("arena", [128, 25600], BF16)))
    oabc = sb([128, 24, SEG], BF16, "oabc")
    merged = sb([128, KC, SEG], BF16, "merged")
    acc = sb([128, SEG], F32, "acc")
    tmpf = Ring([sb([128, 512], F32, f"tmpf{i}") for i in range(2)])
    tmpf2 = Ring([sb([128, SEG], F32, f"tmpf2_{i}") for i in range(2)])
    tmpb = Ring([sb([128, 512], BF16, f"tmpb{i}") for i in range(4)])
    mu = sb([128, SEG], F32, "mu")
    rstd = sb([128, SEG], F32, "rstd")
    PT = [sb([128, SEG], BF16, f"PT{i}") for i in range(2)]
    otok = sb([128, 128], BF16, "otok")
    small = sb([128, 16], F32, "small")
    qst = sb([128, 8, 2], F32, "qst")
    kst = sb([128, NSEQ, 8, 2], F32, "kst")
    ada = sb([128, NSEQ, 96], F32, "ada")
    scb = sb([128, KC, NSEQ], BF16, "scb")
    cTf = sb([128, KC, NSEQ], F32, "cTf")
    badaT = sb([128, 96], F32, "badaT")
    gq = sb([128, 4], F32, "gq")
    gkv = sb([128, 2], F32, "gkv")
    grn = sb([128, 8], F32, "grn")
    ln1 = sb([128, 2, KC], F32, "ln1")
    ln2 = sb([128, 2, KC], F32, "ln2")
    cw = sb([128, FC, 4], F32, "cw")
    ropeC = sb([64, SEG], F32, "ropeC")
    ropeS_ = sb([64, SEG], F32, "ropeS")
    identf = sb([128, 128], F32, "identf")
    identb = sb([128, 128], BF16, "identb")
    onesb = sb([128, 128], BF16, "onesb")
    dm = sb([128, 8, 128], F32, "dm")
    dq = sb([64, 8, 128], F32, "dq")
    dk = sb([128, 2, 8], F32, "dk")
    Sst = sb([64, 8, 128], F32, "Sst")
    Sbf = sb([64, 8, 128], BF16, "Sbf")
    HS = sb([128, FC, 2], F32, "HS")
    ah = Ring([sb([128, SEG + 2], F32, f"ah{i}") for i in range(2)])
    btile = sb([128, SEG], F32, "btile")
    rbt = sb([128, 257], F32, "rbt")
    atm = sb([128, 128], BF16, "atm")
    qd = sb([64, 128], BF16, "qd")
    kdec = sb([128, 64], BF16, "kdec")
    onb = sb([128, 128], BF16, "onb")
    A = arena.t
    OFF_KN, OFF_KR, OFF_V = 0, 8192, 16384

    def kn_v(n0, n1):
        return A[:, OFF_KN + n0:OFF_KN + n1]

    def kr_v(n0, n1):
        return A[0:64, OFF_KR + n0:OFF_KR + n1]

    def v_v(kt, nk, d1=129):
        return A[0:nk, OFF_V + kt * 129:OFF_V + kt * 129 + d1]

    def u_v(j, n):
        return A[:, j * SEG:j * SEG + n]

    pg = [ps([128, 512], F32, f"pg{i}") for i in range(2)]
    pss = [ps([128, 512], F32, f"pss{i}") for i in range(2)]
    po = [ps([128, 512], F32, f"po{i}") for i in range(2)]
    pst = ps([128, 512], F32, "pst")
    ptr = ps([128, 512], BF16, "ptr")
    cnt = {"g": 0, "cv": 0, "w": 0, "s": 0, "p": 0, "wb": 0, "wr": 0}

    def nxt(key, n=2):
        cnt[key] += 1
        return cnt[key] % n

    def cvt_eng():
        cnt["cv"] += 1
        return "dve" if cnt["cv"] % 2 else "act"

    def copy(en, out, in_, R, W):
        if en == "act":
            K.op("act", lambda e: e.copy(out, in_), R, W)
        else:
            K.op(en, lambda e: e.tensor_copy(out, in_), R, W)

    wcache = {}
    lq = {"i": 0}
    WQ = ("sp",)

    def ldq():
        lq["i"] += 1
        return WQ[lq["i"] % len(WQ)]

    def stage_w(Wd, l, k0, nkc, pieces, rot=False, cache=True):
        key = (Wd.name, l, k0, nkc, tuple(pieces), rot)
        c = sum(w_ for _, w_ in pieces)
        bf = wbf[nxt("wb", 4)]
        if cache and key in wcache:
            sc, scr = wcache[key]
            K.dma(ldq(), bf.t[:, 0:nkc, 0:c], sc.t[:, :, :], R=[sc], W=[bf])
            rb_ = None
            if rot:
                rb_ = wrotc[0]
                K.dma(ldq(), rb_.t[:, 0:nkc, 0:c], scr.t[:, :, :], R=[scr], W=[rb_])
            return bf, rb_, c
        st = wst[0]
        Wl = Wd[l].rearrange("(kc p) n -> p kc n", p=128)
        c = 0
        for (c0, wd) in pieces:
            K.dma("sp", st.t[:, 0:nkc, c:c + wd], Wl[:, k0:k0 + nkc, c0:c0 + wd], R=[], W=[st])
            c += wd
        copy(cvt_eng(), bf.t[:, 0:nkc, 0:c], st.t[:, 0:nkc, 0:c], [st], [bf])
        rb_ = None
        if rot:
            rb_ = wrotc[0]
            g = c // 64
            sv = st.t[:, 0:nkc, 0:c].rearrange("p k (g t f) -> p k g t f", g=g, t=2)
            rv = rb_.t[:, 0:nkc, 0:c].rearrange("p k (g t f) -> p k g t f", g=g, t=2)
            K.op("act", lambda e: e.mul(rv[:, :, :, 0, :], sv[:, :, :, 1, :], -1.0), [st], [rb_])
            K.op("dve", lambda e: e.tensor_copy(rv[:, :, :, 1, :], sv[:, :, :, 0, :]), [st], [rb_])
        if cache:
            nm = f"wc{len(wcache)}"
            sc = dscr(nm, [128, nkc, c])
            K.dma("pool", sc.t[:, :, :], bf.t[:, 0:nkc, 0:c], R=[bf], W=[sc])
            scr = None
            if rot:
                scr = dscr(nm + "r", [128, nkc, c])
                K.dma("pool", scr.t[:, :, :], rb_.t[:, 0:nkc, 0:c], R=[rb_], W=[scr])
            wcache[key] = (sc, scr)
        return bf, rb_, c

    def mm_fm(pt, m, wb, nkc, rhs_fn, n, R, first=True, last=True):
        def f(e):
            ins = None
            for kc in range(nkc):
                ins = e.matmul(pt.t[0:m, 0:n], wb.t[:, kc, 0:m], rhs_fn(kc),
                               start=(first and kc == 0), stop=(last and kc == nkc - 1))
            return ins
        K.op("pe", f, [wb] + list(R), [pt])

    def gemm_fm(Wd, l, Kdim, pieces, rhs_buf, rhs_fn, n, rot=False, cache=True):
        nk = Kdim // 128
        pt = pg[nxt("g")]
        pr = None
        k0 = 0
        while k0 < nk:
            nkc = min(KC, nk - k0)
            wb, wr_, m = stage_w(Wd, l, k0, nkc, pieces, rot, cache)
            mm_fm(pt, m, wb, nkc, lambda kc, k0=k0: rhs_fn(k0 + kc), n, [rhs_buf],
                  first=(k0 == 0), last=(k0 + nkc == nk))
            if rot:
                pr = pss[nxt("s")]
                mm_fm(pr, m, wr_, nkc, lambda kc, k0=k0: rhs_fn(k0 + kc), n, [rhs_buf])
            k0 += nkc
        return pt, pr, m

    def gemm_tm(Wd, l, Kdim, pieces, lhs_buf, lhs_fn, ntok, epi):
        nk = Kdim // 128
        Wl = Wd[l].rearrange("(kc p) n -> p kc n", p=128)
        wb = wbf4[0]
        c = sum(w_ for _, w_ in pieces)
        key = ("tm", Wd.name, l, tuple(pieces))
        if key in wcache:
            K.dma(ldq(), wb.t[:, 0:nk, 0:c], wcache[key].t[:, :, :], R=[wcache[key]], W=[wb])
        else:
            for k0 in range(0, nk, 2):
                c = 0
                for (c0, wd) in pieces:
                    K.dma("sp", wst4.t[:, 0:2, c:c + wd], Wl[:, k0:k0 + 2, c0:c0 + wd], R=[], W=[wst4])
                    c += wd
                copy(cvt_eng(), wb.t[:, k0:k0 + 2, 0:c], wst4.t[:, 0:2, 0:c], [wst4], [wb])
            sc = dscr(f"wc{len(wcache)}", [128, nk, c])
            K.dma("pool", sc.t[:, :, :], wb.t[:, 0:nk, 0:c], R=[wb], W=[sc])
            wcache[key] = sc
        for tb in range((ntok + 127) // 128):
            nt = min(128, ntok - tb * 128)
            pt = pg[nxt("g")]

            def f(e, tb=tb, nt=nt, pt=pt):
                ins = None
                for kc in range(nk):
                    ins = e.matmul(pt.t[0:nt, 0:c], lhs_fn(kc, tb * 128, nt), wb.t[:, kc, 0:c],
                                   start=(kc == 0), stop=(kc == nk - 1))
                return ins
            K.op("pe", f, [wb, lhs_buf], [pt])
            epi(pt, tb, nt, c)

    def ones_sum(pt, srcs, n, Rb):
        def f(e):
            ins = None
            for i, (ap, k) in enumerate(srcs):
                ins = e.matmul(pt.t[:, 0:n], onesb.t[0:k, :], ap, start=(i == 0), stop=(i == len(srcs) - 1))
            return ins
        K.op("pe", f, [onesb] + list(Rb), [pt])

    def rsqrt_inplace(buf, ap, scale, eps):
        K.op("dve", lambda e: e.tensor_scalar(ap, ap, scale, eps, ALU.mult, ALU.add), [buf], [buf])
        K.op("act", lambda e: e.activation(ap, ap, AF.Sqrt), [buf], [buf])
        K.op("dve", lambda e: e.reciprocal(ap, ap), [buf], [buf])

    K.dma("sp", identf.t[:], I["ident"].ap(), W=[identf])
    copy("dve", identb.t[:], identf.t[:], [identf], [identb])
    K.op("dve", lambda e: e.memset(onesb.t[:], 1.0), [], [onesb])
    K.dma("sp", dm.t[:], I["dm"].ap().rearrange("h m n -> m h n"), W=[dm])
    K.dma("sp", dq.t[:], I["dq"].ap(), W=[dq])
    K.dma("sp", dk.t[:], I["dk"].ap().rearrange("t m h -> m t h"), W=[dk])
    K.dma("sp", cTf.t[:], I["cT"].ap(), W=[cTf])
    K.op("act", lambda e: e.activation(scb.t[:], cTf.t[:], AF.Silu), [cTf], [scb])

    def layer(l):
        last_layer = (l == NL - 1)
        for (tb, nm) in ((badaT, "b_adaT"), (gq, "g_q"), (gkv, "g_kv"), (grn, "g_rn"),
                         (ln1, "ln1"), (ln2, "ln2"), (cw, "cwT")):
            K.dma("sp", tb.t[:], I[nm][l], W=[tb])
        for j in range(96):
            pt, _, m = gemm_fm(I["w_ada"], l, D, [(j * 128, 128)], scb, lambda kc: scb.t[:, kc, :], NSEQ, cache=False)
            K.op("dve", lambda e, pt=pt, j=j: e.tensor_scalar(ada.t[:, :, j], pt.t[:, 0:NSEQ],
                                                              badaT.t[:, j:j + 1], None, ALU.add), [pt, badaT], [ada])
        for base in (16, 32, 64, 80):
            K.op("dve", lambda e, base=base: e.tensor_scalar_add(ada.t[:, :, base:base + 16],
                                                                 ada.t[:, :, base:base + 16], 1.0), [ada], [ada])
        for s in range(NSEQ):
            sequence(l, s)

    def adav(s, base, j):
        return ada.t[:, s, base + j:base + j + 1]

    def sequence(l, s):
        sq = seqs[s]
        last_layer = (l == NL - 1)
        T, past, bpast = sq["T"], sq["past"], sq["bpast"]
        isP = (s == 0)
        b = s - 1
        if l == 0:
            xsrc = (I["xT_p"].ap() if isP else I["xT_s"][b])
            xsrc_buf = Buf()
        else:
            xsrc, xsrc_buf = sq["x1"].t.ap(), sq["x1"]
        if last_layer:
            xdst = (O["yT_p"].ap() if isP else O["yT_s"][b])
            xdst_buf = OB["yT_p"] if isP else OB["yT_s"]
        else:
            xdst, xdst_buf = sq["x1"].t.ap(), sq["x1"]
        knT, krT, vm, kbT, vbd = sq["knT"], sq["krT"], sq["vm"], sq["kbT"], sq["vb"]
        if isP:
            K.op("dve", lambda e: e.memset(Sst.t[:], 0.0), [], [Sst])
            K.op("dve", lambda e: e.memset(HS.t[:], 0.0), [], [HS])
            K.op("dve", lambda e: e.memset(kst.t[:, s], 0.0), [], [kst])
        else:
            K.dma("sp", Sst.t[:], I["sret"][l, b], W=[Sst])
            K.dma("sp", HS.t[:], I["sconvT"][l, b], W=[HS])
            K.op("dve", lambda e: e.memset(kst.t[:, s], 0.0), [], [kst])
            for t0 in range(0, past, SEG):
                K.dma("sp", ckvr.t[:, :, :], I["ckvT_c"][l, b].rearrange("(kc p) t -> p kc t", p=128)[:, :, t0:t0 + SEG],
                      W=[ckvr])
                copy("dve", ckvb.t[:], ckvr.t[:], [ckvr], [ckvb])
                tmpf.adv()
                K.dma("sp", tmpf.t[0:64, 0:SEG], I["krT_c"][l, b][:, t0:t0 + SEG], W=[tmpf])
                tmpb.adv()
                copy("dve", tmpb.t[0:64, 0:SEG], tmpf.t[0:64, 0:SEG], [tmpf], [tmpb])
                K.dma("pool", krT.t[:, t0:t0 + SEG], tmpb.t[0:64, 0:SEG], R=[tmpb], W=[krT])
                kv_up(l, s, t0, SEG, None)
            for h in range(8):
                tmpf.adv()
                K.dma("sp", tmpf.t[:, 0:bpast], I["bkT_c"][l, b, h], W=[tmpf])
                tmpb.adv()
                copy("dve", tmpb.t[:, 0:bpast], tmpf.t[:, 0:bpast], [tmpf], [tmpb])
                K.dma("pool", kbT.t[h, :, 0:bpast], tmpb.t[:, 0:bpast], R=[tmpb], W=[kbT])
                kstat(s, h, 1, [(tmpb.t[:, 0:bpast], 128)], bpast, [tmpb])
            for tb in range(bpast // 128):
                for hh in range(2):
                    tmpf.adv()
                    K.dma("sp", tmpf.t[:, :], I["bv_c"][l, b, tb * 128:(tb + 1) * 128, hh * 512:(hh + 1) * 512], W=[tmpf])
                    tmpb.adv()
                    copy("dve", tmpb.t[:], tmpf.t[:], [tmpf], [tmpb])
                    K.dma("pool", vbd.t[tb * 128:(tb + 1) * 128, hh * 512:(hh + 1) * 512], tmpb.t[:], R=[tmpb], W=[vbd])
        nseg = (T + sq["seg"] - 1) // sq["seg"]
        for sg in range(nseg):
            segment(l, s, sg * sq["seg"], min(sq["seg"], T - sg * sq["seg"]), xsrc, xsrc_buf, xdst, xdst_buf)
        if isP:
            K.dma("pool", O["ret_p"][l], Sst.t[:], R=[Sst], W=[OB["ret_p"]])
            K.dma("pool", O["convT_p"][l], HS.t[:], R=[HS], W=[OB["convT_p"]])
        else:
            K.dma("pool", O["ret_s"][l, b], Sst.t[:], R=[Sst], W=[OB["ret_s"]])
            K.dma("pool", O["convT_s"][l, b], HS.t[:], R=[HS], W=[OB["convT_s"]])

    def kstat(s, h, which, srcs, n, Rb):
        pt = pst
        sqs = []
        for i, (ap, k) in enumerate(srcs):
            dst = tmpb if i == 0 else onb
            dv = dst.t[0:k, 0:n]
            K.op("act", lambda e, dv=dv, ap=ap: e.activation(dv, ap, AF.Square), list(Rb), [dst])
            sqs.append((dv, k, dst))
        ones_sum(pt, [(a, k) for a, k, _ in sqs], n, [d for _, _, d in sqs])
        K.op("dve", lambda e: e.reduce_max(small.t[:, 0:1], pt.t[:, 0:n], AX.X), [pt], [small])
        K.op("dve", lambda e: e.tensor_max(kst.t[:, s, h, which:which + 1], kst.t[:, s, h, which:which + 1],
                                           small.t[:, 0:1]), [small, kst], [kst])

    def kv_up(l, s, t0, n, _):
        sq = seqs[s]
        K.dma("sp", merged.t[0:64, 0, 0:n], sq["krT"].t[:, t0:t0 + n], R=[sq["krT"]], W=[merged])
        K.op("act", lambda e: e.activation(merged.t[0:64, 2, 0:n], merged.t[0:64, 0, 0:n], AF.Square), [merged], [merged])
        for h in range(8):
            pt, _, m = gemm_fm(I["w_ukv"], l, 256, [(h * 256, 128)], ckvb, lambda kc: ckvb.t[:, kc, 0:n], n)
            tmpb.adv()
            copy("act", tmpb.t[:, 0:n], pt.t[:, 0:n], [pt], [tmpb])
            K.dma("pool", sq["knT"].t[h, :, t0:t0 + n], tmpb.t[:, 0:n], R=[tmpb], W=[sq["knT"]])
            kstat2(s, h, n)
        for half in range(2):
            def epi(pt, tb, nt, c, half=half):
                tmpb.adv()
                copy("act", tmpb.t[0:nt, 0:c], pt.t[0:nt, 0:c], [pt], [tmpb])
                K.dma("pool", sq["vm"].t[t0 + tb * 128:t0 + tb * 128 + nt, half * 512:half * 512 + 512],
                      tmpb.t[0:nt, 0:c], R=[tmpb], W=[sq["vm"]])
            gemm_tm(I["w_ukv"], l, 256, [((half * 4 + hh) * 256 + 128, 128) for hh in range(4)], ckvb,
                    lambda kc, c0, nt: ckvb.t[:, kc, c0:c0 + nt], n, epi)

    def kstat2(s, h, n):
        K.op("act", lambda e: e.activation(merged.t[:, 1, 0:n], tmpb.t[:, 0:n], AF.Square), [tmpb], [merged])
        ones_sum(pst, [(merged.t[:, 1, 0:n], 128), (merged.t[0:64, 2, 0:n], 64)], n, [merged])
        K.op("dve", lambda e: e.reduce_max(small.t[:, 0:1], pst.t[:, 0:n], AX.X), [pst], [small])
        K.op("dve", lambda e: e.tensor_max(kst.t[:, s, h, 0:1], kst.t[:, s, h, 0:1], small.t[:, 0:1]),
             [small, kst], [kst])

    def rope_epi(pt, pr, m, n, t0, out_ap, out_buf, scale=None):
        tmpf.adv()
        tmpf2.adv()
        K.op("dve", lambda e: e.tensor_tensor(tmpf.t[0:m, 0:n], pt.t[0:m, 0:n], ropeC.t[0:m, 0:n], ALU.mult),
             [pt, ropeC], [tmpf])
        K.op("dve", lambda e: e.tensor_tensor(tmpf2.t[0:m, 0:n], pr.t[0:m, 0:n], ropeS_.t[0:m, 0:n], ALU.mult),
             [pr, ropeS_], [tmpf2])
        if scale is None:
            K.op("dve", lambda e: e.tensor_tensor(out_ap, tmpf.t[0:m, 0:n], tmpf2.t[0:m, 0:n], ALU.add),
                 [tmpf, tmpf2], [out_buf])
        else:
            K.op("dve", lambda e: e.tensor_tensor(tmpf.t[0:m, 0:n], tmpf.t[0:m, 0:n], tmpf2.t[0:m, 0:n], ALU.add),
                 [tmpf, tmpf2], [tmpf])
            K.op("act", lambda e: e.mul(out_ap, tmpf.t[0:m, 0:n], scale), [tmpf], [out_buf])

    def layernorm_inplace(n, lnp, s, base_a, base_sh):
        for kc in range(KC):
            copy("act", hT.t[:, kc, 0:n], xs.t[:, kc, 0:n], [xs], [hT])
            K.op("dve", lambda e, kc=kc: e.tensor_tensor(merged.t[:, kc, 0:n], xs.t[:, kc, 0:n], xs.t[:, kc, 0:n], ALU.mult),
                 [xs], [merged])
        ones_sum(pst, [(hT.t[:, kc, 0:n], 128) for kc in range(KC)], n, [hT])
        K.op("act", lambda e: e.mul(mu.t[:, 0:n], pst.t[:, 0:n], 1.0 / D), [pst], [mu])
        ones_sum(pst, [(merged.t[:, kc, 0:n], 128) for kc in range(KC)], n, [merged])
        tmpf.adv()
        K.op("dve", lambda e: e.tensor_tensor(tmpf.t[:, 0:n], mu.t[:, 0:n], mu.t[:, 0:n], ALU.mult), [mu], [tmpf])
        K.op("dve", lambda e: e.scalar_tensor_tensor(rstd.t[:, 0:n], pst.t[:, 0:n], 1.0 / D, tmpf.t[:, 0:n],
                                                     ALU.mult, ALU.subtract), [pst, tmpf], [rstd])
        rsqrt_inplace(rstd, rstd.t[:, 0:n], 1.0, EPS)
        for kc in range(KC):
            tmpf.adv()
            tmpf2.adv()
            K.op("dve", lambda e, kc=kc: e.tensor_tensor(tmpf.t[:, 0:n], xs.t[:, kc, 0:n], mu.t[:, 0:n], ALU.subtract),
                 [xs, mu], [tmpf])
            K.op("dve", lambda e, kc=kc: e.tensor_tensor(tmpf2.t[:, 0:n], tmpf.t[:, 0:n], rstd.t[:, 0:n], ALU.mult),
                 [tmpf, rstd], [tmpf2])
            K.op("act", lambda e, kc=kc: e.activation(xs.t[:, kc, 0:n], tmpf2.t[:, 0:n], AF.Identity,
                                                      bias=lnp.t[:, 1, kc:kc + 1], scale=lnp.t[:, 0, kc:kc + 1]),
                 [tmpf2, lnp], [xs])
            if base_a is not None:
                K.op("dve", lambda e, kc=kc: e.tensor_scalar(hT.t[:, kc, 0:n], xs.t[:, kc, 0:n], adav(s, base_a, kc),
                                                             adav(s, base_sh, kc), ALU.mult, ALU.add), [xs, ada], [hT])

    def attention(s, n, p0, which, h, keys_buf, kT_ap_fn, kr_used, vbuf, klo, khi, q_ap, qr_ap, qbufs, scale, out_col, l):
        sq = seqs[s]
        base = (sq["past"] - sq["past"]) if which == 0 else (0)
        nk = khi - klo
        nkt = (nk + 127) // 128
        K.dma("sp", kn_v(0, nk), kT_ap_fn(klo, khi), R=[keys_buf], W=[arena])
        vsrc = vbuf.t[klo - (0 if which == 0 else sq["boff"]):khi - (0 if which == 0 else sq["boff"]), h * 128:(h + 1) * 128]
        nfull = nk // 128
        if nfull:
            vdst = A[:, OFF_V:OFF_V + nfull * 129].rearrange("p (t d) -> p t d", d=129)[:, :, 0:128]
            K.dma("sp", vdst, vsrc[0:nfull * 128, :].rearrange("(t p) d -> p t d", p=128), R=[vbuf], W=[arena])
        if nk % 128:
            kk = nk % 128
            K.dma("sp", v_v(nfull, kk, 128), vsrc[nfull * 128:nk, :], R=[vbuf], W=[arena])
        vall = A[:, OFF_V:OFF_V + nkt * 129].rearrange("p (t d) -> p t d", d=129)
        K.op("dve", lambda e: e.memset(vall[:, :, 128:129], 1.0), [], [arena])
        K.op("dve", lambda e: e.tensor_tensor(small.t[:, 2:3], qst.t[:, h, which:which + 1],
                                              kst.t[:, s, h, which:which + 1], ALU.mult), [qst, kst], [small])
        K.op("act", lambda e: e.activation(small.t[:, 2:3], small.t[:, 2:3], AF.Sqrt), [small], [small])
        if which == 0:
            K.op("dve", lambda e: e.tensor_scalar(small.t[:, 3:4], small.t[:, 2:3], -scale, None, ALU.mult), [small], [small])
        else:
            K.dma("sp", rbt.t[:], I["rbb"][l, h], W=[rbt])
            K.op("dve", lambda e: e.reduce_max(small.t[:, 4:5], rbt.t[:], AX.X), [rbt], [small])
            K.op("dve", lambda e: e.scalar_tensor_tensor(small.t[:, 3:4], small.t[:, 2:3], -scale, small.t[:, 4:5],
                                                         ALU.mult, ALU.subtract), [small], [small])
            K.op("dve", lambda e: e.tensor_tensor(small.t[:, 5:6], small.t[:, 3:4], rbt.t[:, 256:257], ALU.add),
                 [small, rbt], [small])
        nqb = (n + 127) // 128
        pot = [po[0], po[1]]
        poff = [0, 0]
        assert nqb <= 2
        live = []
        for kt in range(nkt):
            kp = klo + kt * 128
            kk = min(128, nk - kt * 128)
            iv = []
            for sub in range(2):
                ck = (kp + 64 * sub) // 64
                lo = 64 * ck - p0
                hi = n if which == 0 else 64 * (ck + 9) - p0
                iv.append((max(0, min(n, lo)), max(0, min(n, hi))))
            if kk <= 64:
                iv[1] = (0, 0)
            if iv[0][0] >= iv[0][1] and iv[1][0] >= iv[1][1]:
                continue
            live.append((kt, kp, kk, iv))
        for idx, (kt, kp, kk, iv) in enumerate(live):
            pS = pss[nxt("s")]

            def f(e, kt=kt, kk=kk, pS=pS):
                ins = e.matmul(pS.t[0:kk, 0:n], A[:, OFF_KN + kt * 128:OFF_KN + kt * 128 + kk], q_ap, start=True, stop=not kr_used)
                if kr_used:
                    ins = e.matmul(pS.t[0:kk, 0:n], A[0:64, OFF_KR + klo + kt * 128:OFF_KR + klo + kt * 128 + kk], qr_ap,
                                   start=False, stop=True)
                return ins
            K.op("pe", f, [arena] + qbufs, [pS])
            P = PT[nxt("p")]
            if which == 0:
                K.op("act", lambda e, P=P, pS=pS, kk=kk: e.activation(P.t[0:kk, 0:n], pS.t[0:kk, 0:n], AF.Exp,
                                                                      bias=small.t[0:kk, 3:4], scale=scale), [pS, small], [P])
            else:
                Dv = p0 - kp
                if Dv >= 256:
                    K.op("act", lambda e, P=P, pS=pS, kk=kk: e.activation(P.t[0:kk, 0:n], pS.t[0:kk, 0:n], AF.Exp,
                                                                          bias=small.t[0:kk, 5:6], scale=scale), [pS, small], [P])
                else:
                    di = BIAS_DS.index(Dv)
                    K.dma("sp", btile.t[:, 0:n], I["btoe"][l, h, di][:, 0:n], W=[btile])
                    tmpf.adv()
                    K.op("dve", lambda e, pS=pS, kk=kk: e.scalar_tensor_tensor(tmpf.t[0:kk, 0:n], pS.t[0:kk, 0:n], scale,
                                                                               btile.t[0:kk, 0:n], ALU.mult, ALU.add),
                         [pS, btile], [tmpf])
                    K.op("act", lambda e, P=P, kk=kk: e.activation(P.t[0:kk, 0:n], tmpf.t[0:kk, 0:n], AF.Exp,
                                                                   bias=small.t[0:kk, 3:4], scale=1.0), [tmpf, small], [P])
            for sub in range(2):
                r0, r1 = 64 * sub, min(kk, 64 * sub + 64)
                if r1 <= r0:
                    continue
                lo, hi = iv[sub]
                if lo >= hi:
                    lo, hi = 0, 0
                if lo > 0:
                    K.op("dve", lambda e, P=P, r0=r0, r1=r1, lo=lo: e.memset(P.t[r0:r1, 0:lo], 0.0), [], [P])
                if hi < n:
                    K.op("dve", lambda e, P=P, r0=r0, r1=r1, hi=hi: e.memset(P.t[r0:r1, hi:n], 0.0), [], [P])

            def g(e, P=P, kt=kt, kk=kk, idx=idx):
                ins = None
                for qb_ in range(nqb):
                    mq = min(128, n - qb_ * 128)
                    ins = e.matmul(pot[qb_].t[0:mq, poff[qb_]:poff[qb_] + 129], P.t[0:kk, qb_ * 128:qb_ * 128 + mq],
                                   v_v(kt, kk), start=(idx == 0), stop=(idx == len(live) - 1))
                return ins
            K.op("pe", g, [P, arena], [po[0], po[1]])
        for qb_ in range(nqb):
            mq = min(128, n - qb_ * 128)
            pp, off = pot[qb_], poff[qb_]
            K.op("dve", lambda e, pp=pp, off=off, mq=mq: e.reciprocal(small.t[0:mq, 8:9], pp.t[0:mq, off + 128:off + 129]),
                 [pp], [small])
            K.op("dve", lambda e, pp=pp, off=off, mq=mq: e.tensor_scalar(otok.t[0:mq, :], pp.t[0:mq, off:off + 128],
                                                                         small.t[0:mq, 8:9], None, ALU.mult), [pp, small], [otok])
            K.op("pe", lambda e, mq=mq: e.transpose(ptr.t[:, 0:mq], otok.t[0:mq, :], identb.t[0:mq, 0:mq]), [otok, identb], [ptr])
            copy("act", oabc.t[:, out_col, qb_ * 128:qb_ * 128 + mq], ptr.t[:, 0:mq], [ptr], [oabc])

    def segment(l, s, t0, n, xsrc, xsrc_buf, xdst, xdst_buf):
        sq = seqs[s]
        isP = (s == 0)
        b = s - 1
        last_layer = (l == NL - 1)
        past, bpast = sq["past"], sq["bpast"]
        p0 = past + t0
        knT, krT, vm, kbT, vbd = sq["knT"], sq["krT"], sq["vm"], sq["kbT"], sq["vb"]
        rp = I["ropeP"] if isP else I["ropeS"]
        K.dma("sp", ropeC.t[:, 0:n], rp[0][:, t0:t0 + n], W=[ropeC])
        K.dma("sp", ropeS_.t[:, 0:n], rp[1][:, t0:t0 + n], W=[ropeS_])
        K.dma("sp", xs.t[:, :, 0:n], xsrc.rearrange("(kc p) t -> p kc t", p=128)[:, :, t0:t0 + n], R=[xsrc_buf], W=[xs])
        for kc in range(KC):
            K.op("dve", lambda e, kc=kc: e.tensor_scalar(hT.t[:, kc, 0:n], xs.t[:, kc, 0:n], adav(s, 16, kc), adav(s, 0, kc),
                                                         ALU.mult, ALU.add), [xs, ada], [hT])
        hfn = lambda kc: hT.t[:, kc, 0:n]
        Win = I["w_in"]
        for j in range(4):
            pt, _, m = gemm_fm(Win, l, D, [(O_CQ + j * 128, 128)], hT, hfn, n)
            copy("act", cq.t[:, j, 0:n], pt.t[:, 0:n], [pt], [cq])
        for j in range(2):
            pt, _, m = gemm_fm(Win, l, D, [(O_CKV + j * 128, 128)], hT, hfn, n)
            copy("act", ckvr.t[:, j, 0:n], pt.t[:, 0:n], [pt], [ckvr])
        pt, pr, m = gemm_fm(Win, l, D, [(O_KR, 64)], hT, hfn, n, rot=True)
        rope_epi(pt, pr, 64, n, t0, acc.t[0:64, 0:n], acc)
        ko = (O["krT_p"][l][:, t0:t0 + n] if isP else O["krT_s"][l, b])
        K.dma("pool", ko, acc.t[0:64, 0:n], R=[acc], W=[OB["krT_p" if isP else "krT_s"]])
        tmpb.adv()
        copy("dve", tmpb.t[0:64, 0:n], acc.t[0:64, 0:n], [acc], [tmpb])
        K.dma("pool", krT.t[:, p0:p0 + n], tmpb.t[0:64, 0:n], R=[tmpb], W=[krT])
        for j in range(2):
            K.op("act", lambda e, j=j: e.activation(merged.t[:, j, 0:n], ckvr.t[:, j, 0:n], AF.Square), [ckvr], [merged])
        ones_sum(pst, [(merged.t[:, j, 0:n], 128) for j in range(2)], n, [merged])
        copy("act", rstd.t[:, 0:n], pst.t[:, 0:n], [pst], [rstd])
        rsqrt_inplace(rstd, rstd.t[:, 0:n], 1.0 / 256, EPS)
        for j in range(2):
            K.op("dve", lambda e, j=j: e.scalar_tensor_tensor(ckvr.t[:, j, 0:n], ckvr.t[:, j, 0:n], gkv.t[:, j:j + 1],
                                                              rstd.t[:, 0:n], ALU.mult, ALU.mult), [ckvr, gkv, rstd], [ckvr])
            copy("act", ckvb.t[:, j, 0:n], ckvr.t[:, j, 0:n], [ckvr], [ckvb])
        co = (O["ckvT_p"][l].rearrange("(kc p) t -> p kc t", p=128)[:, :, t0:t0 + n] if isP
              else O["ckvT_s"][l, b].rearrange("(kc p) t -> p kc t", p=128))
        K.dma("pool", co, ckvr.t[:, :, 0:n], R=[ckvr], W=[OB["ckvT_p" if isP else "ckvT_s"]])
        kv_up(l, s, p0, n, None)
        for j in range(4):
            K.op("act", lambda e, j=j: e.activation(merged.t[:, j, 0:n], cq.t[:, j, 0:n], AF.Square), [cq], [merged])
        ones_sum(pst, [(merged.t[:, j, 0:n], 128) for j in range(4)], n, [merged])
        copy("act", rstd.t[:, 0:n], pst.t[:, 0:n], [pst], [rstd])
        rsqrt_inplace(rstd, rstd.t[:, 0:n], 1.0 / 512, EPS)
        for j in range(4):
            K.op("dve", lambda e, j=j: e.scalar_tensor_tensor(cqn.t[:, j, 0:n], cq.t[:, j, 0:n], gq.t[:, j:j + 1],
                                                              rstd.t[:, 0:n], ALU.mult, ALU.mult), [cq, gq, rstd], [cqn])
        cfn = lambda kc: cqn.t[:, kc, 0:n]
        K.op("dve", lambda e: e.memset(qst.t[:], 0.0), [], [qst])
        for h in range(8):
            pt, _, m = gemm_fm(I["w_uq"], l, 512, [(h * 192, 128)], cqn, cfn, n)
            tmpb.adv()
            copy("act", tmpb.t[:, 0:n], pt.t[:, 0:n], [pt], [tmpb])
            K.dma("pool", qscr.t[1, h, :, 0:n], tmpb.t[:, 0:n], R=[tmpb], W=[qscr])
            pt2, pr2, m = gemm_fm(I["w_uq"], l, 512, [(h * 192 + 128, 64)], cqn, cfn, n, rot=True)
            rope_epi(pt2, pr2, 64, n, t0, merged.t[0:64, 0, 0:n], merged)
            K.dma("pool", qscr.t[2, h, 0:64, 0:n], merged.t[0:64, 0, 0:n], R=[merged], W=[qscr])
            K.op("act", lambda e: e.activation(merged.t[:, 1, 0:n], tmpb.t[:, 0:n], AF.Square), [tmpb], [merged])
            K.op("act", lambda e: e.activation(merged.t[0:64, 2, 0:n], merged.t[0:64, 0, 0:n], AF.Square), [merged], [merged])
            ones_sum(pst, [(merged.t[:, 1, 0:n], 128), (merged.t[0:64, 2, 0:n], 64)], n, [merged])
            K.op("dve", lambda e, h=h: e.reduce_max(qst.t[:, h, 0:1], pst.t[:, 0:n], AX.X), [pst], [qst])
        for h in range(8):
            pt, _, m = gemm_fm(Win, l, D, [(O_QB + h * 128, 128)], hT, hfn, n)
            tmpb.adv()
            copy("act", tmpb.t[:, 0:n], pt.t[:, 0:n], [pt], [tmpb])
            K.dma("pool", qscr.t[0, h, :, 0:n], tmpb.t[:, 0:n], R=[tmpb], W=[qscr])
            K.op("act", lambda e: e.activation(merged.t[:, 1, 0:n], tmpb.t[:, 0:n], AF.Square), [tmpb], [merged])
            ones_sum(pst, [(merged.t[:, 1, 0:n], 128)], n, [merged])
            K.op("dve", lambda e, h=h: e.reduce_max(qst.t[:, h, 1:2], pst.t[:, 0:n], AX.X), [pst], [qst])
            pt, _, m = gemm_fm(Win, l, D, [(O_KB + h * 128, 128)], hT, hfn, n)
            tmpb.adv()
            copy("act", tmpb.t[:, 0:n], pt.t[:, 0:n], [pt], [tmpb])
            K.dma("pool", kbT.t[h, :, bpast + t0:bpast + t0 + n], tmpb.t[:, 0:n], R=[tmpb], W=[kbT])
            kstat(s, h, 1, [(tmpb.t[:, 0:n], 128)], n, [tmpb])
            if isP:
                if t0 + n > sq["T"] - KEEP:
                    tmpf.adv()
                    copy("act", tmpf.t[:, 0:n], pt.t[:, 0:n], [pt], [tmpf])
                    o0 = t0 - (sq["T"] - KEEP)
                    K.dma("pool", O["bkT_p"][l, h][:, o0:o0 + n], tmpf.t[:, 0:n], R=[tmpf], W=[OB["bkT_p"]])
            else:
                tmpf.adv()
                copy("act", tmpf.t[:, 0:n], pt.t[:, 0:n], [pt], [tmpf])
                K.dma("pool", O["bkT_s"][l, b, h], tmpf.t[:, 0:n], R=[tmpf], W=[OB["bkT_s"]])
        for half in range(2):
            def epi(pt, tb, nt, c, half=half):
                tmpb.adv()
                copy("act", tmpb.t[0:nt, 0:c], pt.t[0:nt, 0:c], [pt], [tmpb])
                r0 = bpast + t0 + tb * 128
                K.dma("pool", vbd.t[r0:r0 + nt, half * 512:half * 512 + 512], tmpb.t[0:nt, 0:c], R=[tmpb], W=[vbd])
                if isP:
                    if t0 + n > sq["T"] - KEEP:
                        tmpf.adv()
                        copy("act", tmpf.t[0:nt, 0:c], pt.t[0:nt, 0:c], [pt], [tmpf])
                        o0 = t0 - (sq["T"] - KEEP) + tb * 128
                        K.dma("pool", O["bv_p"][l][o0:o0 + nt, half * 512:half * 512 + 512], tmpf.t[0:nt, 0:c],
                              R=[tmpf], W=[OB["bv_p"]])
                else:
                    tmpf.adv()
                    copy("act", tmpf.t[0:nt, 0:c], pt.t[0:nt, 0:c], [pt], [tmpf])
                    K.dma("pool", O["bv_s"][l, b][tb * 128:tb * 128 + nt, half * 512:half * 512 + 512], tmpf.t[0:nt, 0:c],
                          R=[tmpf], W=[OB["bv_s"]])
            gemm_tm(Win, l, D, [(O_VB + half * 512, 512)], hT, lambda kc, c0, nt: hT.t[:, kc, c0:c0 + nt], n, epi)
        for h in range(8):
            pt, pr, m = gemm_fm(Win, l, D, [(O_QR + h * 64, 64)], hT, hfn, n, rot=True)
            rope_epi(pt, pr, 64, n, t0, qret.t[:, h, 0:n], qret, scale=64 ** -0.5)
            pt, pr, m = gemm_fm(Win, l, D, [(O_KRR + h * 64, 64)], hT, hfn, n, rot=True)
            rope_epi(pt, pr, 64, n, t0, kret.t[:, h, 0:n], kret)
        for half in range(2):
            def epi(pt, tb, nt, c, half=half):
                copy("act", vr.t[0:nt, tb, half * 512:half * 512 + 512], pt.t[0:nt, 0:c], [pt], [vr])
            gemm_tm(Win, l, D, [(O_VR + half * 512, 512)], hT, lambda kc, c0, nt: hT.t[:, kc, c0:c0 + nt], n, epi)
        for j in range(8):
            pt, _, m = gemm_fm(Win, l, D, [(O_GR + j * 128, 128)], hT, hfn, n)
            K.op("act", lambda e, pt=pt, j=j: e.activation(sgr.t[:, j, 0:n], pt.t[:, 0:n], AF.Silu), [pt], [sgr])
        for j in range(48):
            pt, _, m = gemm_fm(Win, l, D, [(O_GA + j * 128, 128)], hT, hfn, n)
            tmpb.adv()
            K.op("act", lambda e, pt=pt: e.activation(tmpb.t[:, 0:n], pt.t[:, 0:n], AF.Sigmoid), [pt], [tmpb])
            K.dma("pool", gates.t[j, :, 0:n], tmpb.t[:, 0:n], R=[tmpb], W=[gates])
        khi = p0 + n
        K.dma("sp", kr_v(0, khi), krT.t[:, 0:khi], R=[krT], W=[arena])
        for h in range(8):
            K.dma("sp", merged.t[:, 4, 0:n], qscr.t[1, h, :, 0:n], R=[qscr], W=[merged])
            K.dma("sp", merged.t[0:64, 5, 0:n], qscr.t[2, h, 0:64, 0:n], R=[qscr], W=[merged])
            attention(s, n, p0, 0, h, knT, lambda a, b_, h=h: knT.t[h, :, a:b_], True, vm, 0, khi,
                      merged.t[:, 4, 0:n], merged.t[0:64, 5, 0:n], [merged], SC_MLA, h, l)
        sq["boff"] = past - bpast
        blo = max(sq["boff"], p0 - 512)
        for h in range(8):
            K.dma("sp", merged.t[:, 4, 0:n], qscr.t[0, h, :, 0:n], R=[qscr], W=[merged])
            attention(s, n, p0, 1, h, kbT, lambda a, b_, h=h: kbT.t[h, :, a - sq["boff"]:b_ - sq["boff"]], False, vbd,
                      blo, khi, merged.t[:, 4, 0:n], None, [merged], SC_B, 8 + h, l)
        Lc = 128 if n >= 128 else n
        dki = 0 if Lc == 128 else 1
        for c in range(n // Lc):
            c0 = c * Lc
            for h in range(8):
                gam = 1.0 - 2.0 ** (-5 - h)
                gL = float(gam ** Lc)
                K.op("pe", lambda e, h=h, c0=c0: e.matmul(pss[0].t[0:Lc, 0:Lc], kret.t[:, h, c0:c0 + Lc], qret.t[:, h, c0:c0 + Lc],
                                                          start=True, stop=True), [kret, qret], [pss[0]])
                K.op("dve", lambda e, h=h: e.tensor_tensor(atm.t[0:Lc, 0:Lc], pss[0].t[0:Lc, 0:Lc], dm.t[0:Lc, h, 0:Lc], ALU.mult),
                     [pss[0], dm], [atm])
                K.op("dve", lambda e, h=h, c0=c0: e.tensor_tensor(qd.t[:, 0:Lc], qret.t[:, h, c0:c0 + Lc], dq.t[:, h, 0:Lc], ALU.mult),
                     [qret, dq], [qd])
                copy("act", Sbf.t[:, h, :], Sst.t[:, h, :], [Sst], [Sbf])

                def fo(e, h=h, c=c):
                    e.matmul(pss[1].t[0:Lc, 0:128], atm.t[0:Lc, 0:Lc], vr.t[0:Lc, c, h * 128:(h + 1) * 128], start=True, stop=False)
                    return e.matmul(pss[1].t[0:Lc, 0:128], qd.t[:, 0:Lc], Sbf.t[:, h, :], start=False, stop=True)
                K.op("pe", fo, [atm, vr, qd, Sbf], [pss[1]])
                K.op("pe", lambda e, h=h, c0=c0: e.transpose(ptr.t[0:Lc, 0:64], kret.t[:, h, c0:c0 + Lc], identb.t[0:64, 0:64]),
                     [kret, identb], [ptr])
                K.op("dve", lambda e, h=h: e.tensor_scalar(kdec.t[0:Lc, :], ptr.t[0:Lc, 0:64], dk.t[0:Lc, dki, h:h + 1], None, ALU.mult),
                     [ptr, dk], [kdec])
                K.op("pe", lambda e, h=h, c=c: e.matmul(pst.t[0:64, 0:128], kdec.t[0:Lc, :], vr.t[0:Lc, c, h * 128:(h + 1) * 128],
                                                        start=True, stop=True), [kdec, vr], [pst])
                K.op("dve", lambda e, h=h: e.scalar_tensor_tensor(Sst.t[:, h, :], Sst.t[:, h, :], gL, pst.t[0:64, 0:128],
                                                                  ALU.mult, ALU.add), [Sst, pst], [Sst])
                tmpf.adv()
                tmpf2.adv()
                K.op("act", lambda e: e.activation(tmpf.t[0:Lc, 0:128], pss[1].t[0:Lc, 0:128], AF.Identity,
                                                   accum_out=small.t[0:Lc, 10:11]), [pss[1]], [tmpf, small])
                K.op("act", lambda e: e.activation(tmpf2.t[0:Lc, 0:128], pss[1].t[0:Lc, 0:128], AF.Square,
                                                   accum_out=small.t[0:Lc, 11:12]), [pss[1]], [tmpf2, small])
                K.op("dve", lambda e: e.tensor_scalar(small.t[0:Lc, 10:12], small.t[0:Lc, 10:12], 1.0 / 128, None, ALU.mult),
                     [small], [small])
                K.op("dve", lambda e: e.tensor_tensor(small.t[0:Lc, 12:13], small.t[0:Lc, 10:11], small.t[0:Lc, 10:11], ALU.mult),
                     [small], [small])
                K.op("dve", lambda e: e.tensor_tensor(small.t[0:Lc, 13:14], small.t[0:Lc, 11:12], small.t[0:Lc, 12:13], ALU.subtract),
                     [small], [small])
                rsqrt_inplace(small, small.t[0:Lc, 13:14], 1.0, EPS)
                K.op("dve", lambda e: e.tensor_scalar(onb.t[0:Lc, :], tmpf.t[0:Lc, 0:128], small.t[0:Lc, 10:11], small.t[0:Lc, 13:14],
                                                      ALU.subtract, ALU.mult), [tmpf, small], [onb])
                K.op("pe", lambda e: e.transpose(ptr.t[:, 0:Lc], onb.t[0:Lc, :], identb.t[0:Lc, 0:Lc]), [onb, identb], [ptr])
                K.op("dve", lambda e, h=h, c0=c0: e.scalar_tensor_tensor(oabc.t[:, 16 + h, c0:c0 + Lc], ptr.t[:, 0:Lc], grn.t[:, h:h + 1],
                                                                         sgr.t[:, h, c0:c0 + Lc], ALU.mult, ALU.mult),
                     [ptr, grn, sgr], [oabc])
        for j in range(16):
            for bi, Wn in enumerate(("w_pa", "w_pb", "w_pc")):
                pt, _, m = gemm_fm(I[Wn], l, 1024, [(j * 128, 128)], oabc, lambda kc, bi=bi: oabc.t[:, bi * 8 + kc, 0:n], n)
                tmpb.adv()
                K.dma("sp", tmpb.t[:, 0:n], gates.t[bi * 16 + j, :, 0:n], R=[gates], W=[tmpb])
                if bi == 0:
                    K.op("dve", lambda e, pt=pt: e.tensor_tensor(acc.t[:, 0:n], pt.t[:, 0:n], tmpb.t[:, 0:n], ALU.mult),
                         [pt, tmpb], [acc])
                else:
                    tmpf.adv()
                    K.op("dve", lambda e, pt=pt: e.tensor_tensor(tmpf.t[:, 0:n], pt.t[:, 0:n], tmpb.t[:, 0:n], ALU.mult),
                         [pt, tmpb], [tmpf])
                    K.op("dve", lambda e: e.tensor_tensor(acc.t[:, 0:n], acc.t[:, 0:n], tmpf.t[:, 0:n], ALU.add),
                         [acc, tmpf], [acc])
            copy("act", merged.t[:, j, 0:n], acc.t[:, 0:n], [acc], [merged])
        for j in range(16):
            pt, _, m = gemm_fm(I["w_o"], l, D, [(j * 128, 128)], merged, lambda kc: merged.t[:, kc, 0:n], n)
            tmpf.adv()
            K.op("dve", lambda e, pt=pt, j=j: e.tensor_scalar(tmpf.t[:, 0:n], pt.t[:, 0:n], adav(s, 32, j), None, ALU.mult),
                 [pt, ada], [tmpf])
            K.op("dve", lambda e, j=j: e.scalar_tensor_tensor(xs.t[:, j, 0:n], xs.t[:, j, 0:n], ALPHA, tmpf.t[:, 0:n],
                                                              ALU.mult, ALU.add), [xs, tmpf], [xs])
        layernorm_inplace(n, ln1, s, 64, 48)
        for jb in range(FC):
            pa, _, m = gemm_fm(I["w_fa"], l, D, [(jb * 128, 128)], hT, hfn, n)
            ah.adv()
            tmpf.adv()
            tmpf2.adv()
            copy("dve", ah.t[:, 0:2], HS.t[:, jb, :], [HS], [ah])
            copy("act", ah.t[:, 2:2 + n], pa.t[:, 0:n], [pa, ah], [ah])
            copy("dve", HS.t[:, jb, :], ah.t[:, n:n + 2], [ah], [HS])
            K.op("dve", lambda e, jb=jb: e.tensor_scalar(tmpf.t[:, 0:n], ah.t[:, 0:n], cw.t[:, jb, 0:1], cw.t[:, jb, 3:4],
                                                         ALU.mult, ALU.add), [ah, cw], [tmpf])
            K.op("dve", lambda e, jb=jb: e.scalar_tensor_tensor(tmpf.t[:, 0:n], ah.t[:, 1:1 + n], cw.t[:, jb, 1:2], tmpf.t[:, 0:n],
                                                                ALU.mult, ALU.add), [ah, cw, tmpf], [tmpf])
            K.op("dve", lambda e, jb=jb: e.scalar_tensor_tensor(tmpf.t[:, 0:n], ah.t[:, 2:2 + n], cw.t[:, jb, 2:3], tmpf.t[:, 0:n],
                                                                ALU.mult, ALU.add), [ah, cw, tmpf], [tmpf])
            K.op("act", lambda e: e.activation(tmpf2.t[:, 0:n], tmpf.t[:, 0:n], AF.Gelu), [tmpf], [tmpf2])
            pb, _, m = gemm_fm(I["w_fb"], l, D, [(jb * 128, 128)], hT, hfn, n)
            K.op("dve", lambda e, pb=pb, jb=jb: e.tensor_tensor(u_v(jb, n), tmpf2.t[:, 0:n], pb.t[:, 0:n], ALU.mult),
                 [tmpf2, pb], [arena])
        for j in range(16):
            pt, _, m = gemm_fm(I["w_fd"], l, DFF, [(j * 128, 128)], arena, lambda kc: u_v(kc, n), n)
            tmpf.adv()
            K.op("dve", lambda e, pt=pt, j=j: e.tensor_scalar(tmpf.t[:, 0:n], pt.t[:, 0:n], adav(s, 80, j), None, ALU.mult),
                 [pt, ada], [tmpf])
            K.op("dve", lambda e, j=j: e.scalar_tensor_tensor(xs.t[:, j, 0:n], xs.t[:, j, 0:n], ALPHA, tmpf.t[:, 0:n],
                                                              ALU.mult, ALU.add), [xs, tmpf], [xs])
        layernorm_inplace(n, ln2, s, None, None)
        K.dma("pool", xdst.rearrange("(kc p) t -> p kc t", p=128)[:, :, t0:t0 + n], xs.t[:, :, 0:n], R=[xs], W=[xdst_buf])

    for l in range(NL):
        layer(l)
    for q in ("sp", "pool", "pe", "act"):
        for sl in K.slots[q]:
            if sl.val:
                K.E["pool"].h.wait_ge(sl.sem, sl.val)
    for en in ("pe", "act", "dve"):
        e = K.E[en]
        if e.cnt:
            K.E["pool"].h.wait_ge(e.sem, e.cnt)
    return nc, es


def _consts(TP, PAST):
    half = 32
    inv = (10000.0 ** (-np.arange(half, dtype=np.float32) / half)).astype(np.float32)

    def tab(pos):
        ang = pos.astype(np.float32)[:, None] * inv[None, :]
        c, s_ = np.cos(ang).astype(np.float32), np.sin(ang).astype(np.float32)
        return np.stack([np.concatenate([c, c], 1).T, np.concatenate([s_, s_], 1).T]).astype(np.float32)
    ropeP = np.ascontiguousarray(tab(np.arange(TP)))
    ropeS = np.ascontiguousarray(tab(PAST + np.arange(16)))
    lg = np.log1p(-np.exp2(-5.0 - np.arange(8, dtype=np.float32))).astype(np.float32)
    idx = np.arange(128, dtype=np.float32)
    diff = idx[None, :] - idx[:, None]
    dm = np.where(diff[None] >= 0, np.exp(np.maximum(diff, 0.0)[None] * lg[:, None, None]), 0.0).astype(np.float32)
    dq = np.exp((idx[None, :] + 1.0) * lg[:, None]).astype(np.float32)
    dq = np.ascontiguousarray(np.broadcast_to(dq[None], (64, 8, 128))).astype(np.float32)
    dk = np.stack([np.exp((127.0 - idx)[:, None] * lg[None, :]),
                   np.exp((15.0 - idx)[:, None] * lg[None, :])]).astype(np.float32)
    dk[1, 16:] = 0.0
    return dict(ropeP=ropeP, ropeS=ropeS, dm=dm, dq=dq, dk=np.ascontiguousarray(dk),
                ident=np.eye(128, dtype=np.float32))


def _vec128(v):
    L_ = v.shape[0]
    return np.ascontiguousarray(v.reshape(L_, -1, 128).transpose(0, 2, 1))


def host_inputs(inp, core, TP, NSB, NL, PAST=1024, BPAST=512):
    f = np.float32
    pb = core % inp["x_prompt"].shape[0]
    sb = [(core * NSB + i) % inp["x_sample"].shape[0] for i in range(NSB)]
    m = {}
    m["xT_p"] = np.ascontiguousarray(inp["x_prompt"][pb, :TP].T)
    m["xT_s"] = np.ascontiguousarray(inp["x_sample"][sb].transpose(0, 2, 1))
    c = np.concatenate([inp["c_prompt"][pb:pb + 1], inp["c_sample"][sb]], 0)
    m["cT"] = np.ascontiguousarray(c.reshape(-1, KC, 128).transpose(2, 1, 0))
    m["ckvT_c"] = np.ascontiguousarray(inp["cache_mla_ckv"][:NL][:, sb].transpose(0, 1, 3, 2))
    m["krT_c"] = np.ascontiguousarray(inp["cache_mla_krope"][:NL][:, sb].transpose(0, 1, 3, 2))
    m["bkT_c"] = np.ascontiguousarray(inp["cache_band_k"][:NL][:, sb].transpose(0, 1, 3, 4, 2))
    m["bv_c"] = np.ascontiguousarray(inp["cache_band_v"][:NL][:, sb].reshape(NL, NSB, BPAST, 1024))
    m["sret"] = np.ascontiguousarray(inp["state_ret"][:NL][:, sb].transpose(0, 1, 3, 2, 4))
    sc = inp["state_conv"][:NL][:, sb]
    m["sconvT"] = np.ascontiguousarray(sc.reshape(NL, NSB, 2, FC, 128).transpose(0, 1, 4, 3, 2))
    return m


def shared_inputs(inp, TP, NL, PAST=1024):
    m = {}
    for k in ("w_ada", "w_in", "w_uq", "w_ukv", "w_o"):
        m[k] = np.ascontiguousarray(inp[k][:NL])
    m["w_pa"] = np.ascontiguousarray(inp["w_branch_a"][:NL])
    m["w_pb"] = np.ascontiguousarray(inp["w_branch_b"][:NL])
    m["w_pc"] = np.ascontiguousarray(inp["w_branch_c"][:NL])
    m["w_fa"] = np.ascontiguousarray(inp["w_ff_a"][:NL])
    m["w_fb"] = np.ascontiguousarray(inp["w_ff_b"][:NL])
    m["w_fd"] = np.ascontiguousarray(inp["w_ff_down"][:NL])
    m["b_adaT"] = _vec128(inp["b_ada"][:NL])
    m["g_q"] = _vec128(inp["g_q_lora"][:NL])
    m["g_kv"] = _vec128(inp["g_kv_lora"][:NL])
    m["g_rn"] = _vec128(inp["g_ret_norm"][:NL])
    m["ln1"] = np.ascontiguousarray(np.stack([_vec128(inp["ln1_g"][:NL]), _vec128(inp["ln1_b"][:NL])], 2))
    m["ln2"] = np.ascontiguousarray(np.stack([_vec128(inp["ln2_g"][:NL]), _vec128(inp["ln2_b"][:NL])], 2))
    cwf = np.concatenate([inp["conv_w"][:NL], inp["conv_b"][:NL][:, None, :]], 1)
    m["cwT"] = np.ascontiguousarray(cwf.reshape(NL, 4, FC, 128).transpose(0, 3, 2, 1))
    rb = inp["rel_bias"][:NL]
    m["rbb"] = np.ascontiguousarray(np.broadcast_to(rb[:, :, None, :], (NL, 8, 128, 257)))
    r = np.arange(128)[:, None]
    i = np.arange(512)[None, :]
    idxs = np.stack([np.clip(Dv + i - r, -128, 128) + 128 for Dv in BIAS_DS])
    m["btoe"] = np.ascontiguousarray(rb[:, :, idxs])
    m.update(_consts(TP, PAST))
    return m


def assemble(res, TP, NSB, NL, nP, nS):
    KEEP = min(512, TP)
    P = res[:nP]
    ncs = nS // NSB

    def cat_s(f_):
        return np.concatenate([f_(res[c]) for c in range(ncs)], 1)
    y_p = np.stack([r["yT_p"].T for r in P])
    y_s = np.concatenate([r["yT_s"].transpose(0, 2, 1) for r in res[:ncs]], 0)
    ckv_p = np.stack([r["ckvT_p"].transpose(0, 2, 1) for r in P], 1)
    kr_p = np.stack([r["krT_p"].transpose(0, 2, 1) for r in P], 1)
    bk_p = np.stack([r["bkT_p"].transpose(0, 3, 1, 2) for r in P], 1)
    bv_p = np.stack([r["bv_p"].reshape(NL, KEEP, 8, 128) for r in P], 1)
    ret_p = np.stack([r["ret_p"].transpose(0, 2, 1, 3) for r in P], 1)
    conv_p = np.stack([r["convT_p"].transpose(0, 3, 2, 1).reshape(NL, 2, DFF) for r in P], 1)
    ckv_s = cat_s(lambda r: r["ckvT_s"].transpose(0, 1, 3, 2))
    kr_s = cat_s(lambda r: r["krT_s"].transpose(0, 1, 3, 2))
    bk_s = cat_s(lambda r: r["bkT_s"].transpose(0, 1, 4, 2, 3))
    bv_s = cat_s(lambda r: r["bv_s"].reshape(NL, NSB, 16, 8, 128))
    ret_s = cat_s(lambda r: r["ret_s"].transpose(0, 1, 3, 2, 4))
    conv_s = cat_s(lambda r: r["convT_s"].transpose(0, 1, 4, 3, 2).reshape(NL, NSB, 2, DFF))
    outs = (y_p, y_s, ckv_p, kr_p, bk_p, bv_p, ret_p, conv_p, ckv_s, kr_s, bk_s, bv_s, ret_s, conv_s)
    return tuple(np.ascontiguousarray(o, dtype=np.float32) for o in outs)


def kernel(**inputs):
    inp = {k: np.asarray(v) for k, v in inputs.items()}
    NL = inp["w_in"].shape[0]
    TP = inp["x_prompt"].shape[1]
    nP = inp["x_prompt"].shape[0]
    nS = inp["x_sample"].shape[0]
    NSB = nS // 8
    PAST = inp["cache_mla_ckv"].shape[2]
    nc, es = build(TP, NSB, NL, PAST=PAST, BPAST=inp["cache_band_k"].shape[2])
    shared = shared_inputs(inp, TP, NL, PAST)
    in_maps = []
    for c in range(8):
        m = dict(shared)
        m.update(host_inputs(inp, c, TP, NSB, NL, PAST, inp["cache_band_k"].shape[2]))
        in_maps.append(m)
    res = run_bass_kernel_spmd(nc, in_maps, core_ids=list(range(8)))
    es.close()
    return assemble(res.results, TP, NSB, NL, nP, nS)
```
